# Optimizing a Trainium2 kernel written in Bass

```python
import math
import jax, jax.numpy as jnp
from jax import lax
import numpy as np

D_MODEL = 2048
BATCH = 2
SEQ = 4096
DEPTH = 1
DEC_BATCH = 8
DEC_SEQ = 1
PAST_LEN = 16384
PAGE_SIZE = 128

HEAD_DIM = 128
WIDTH_A = D_MODEL // 2
HEADS_A = WIDTH_A // HEAD_DIM
DILATED = ((128, 1), (512, 4), (2048, 16))
WIN_MAX = 2048
N_BUCKETS = 32
MAX_DISTANCE = WIN_MAX
DK = 128
DV = 128
V_HEADS_B = (D_MODEL - WIDTH_A) // DV
QK_HEADS_B = V_HEADS_B // 2
WIDTH_BQK = QK_HEADS_B * DK
WIDTH_BV = V_HEADS_B * DV
CONV_W = 4
CONV_DIM = 2 * WIDTH_BQK + WIDTH_BV
CHUNK = 64
D_FF = 5632
FFN_CONV_W = 3
EPS = 1e-6
NEG = -1e30

SPLITS = (WIDTH_A, WIDTH_A, WIDTH_A, WIDTH_BQK, WIDTH_BQK, WIDTH_BV, WIDTH_BV, V_HEADS_B, V_HEADS_B)
SPLIT_IDX = tuple(sum(SPLITS[:i + 1]) for i in range(len(SPLITS) - 1))
PROJ_DIM = sum(SPLITS)
MIX_OUT = WIDTH_A + WIDTH_BV

kernel_name = "hybrid_dilated_swa_gated_deltanet_convffn_step"


def rmsnorm(x, w):
    xf = x.astype(jnp.float32)
    y = xf * lax.rsqrt(jnp.mean(xf * xf, axis=-1, keepdims=True) + EPS)
    return (y * w.astype(jnp.float32)).astype(x.dtype)


def l2norm(x):
    xf = x.astype(jnp.float32)
    return xf * lax.rsqrt(jnp.sum(xf * xf, axis=-1, keepdims=True) + EPS)


def rel_bucket(dist):
    max_exact = N_BUCKETS // 2
    d = jnp.maximum(dist, 1).astype(jnp.float32)
    large = max_exact + (jnp.log(d / max_exact) / math.log(MAX_DISTANCE / max_exact)
                         * (N_BUCKETS - max_exact)).astype(jnp.int32)
    large = jnp.minimum(large, N_BUCKETS - 1)
    return jnp.where(dist < max_exact, dist, large)


def causal_dwconv(u, buf, w):
    full = jnp.concatenate([buf.astype(u.dtype), u], axis=1)
    out = lax.conv_general_dilated(full, w[:, None, :].astype(u.dtype), (1,), 'VALID',
                                   dimension_numbers=('NWC', 'WIO', 'NWC'),
                                   feature_group_count=u.shape[-1])
    return out, full[:, full.shape[1] - (w.shape[0] - 1):]


def dilated_branch_prompt(q, k, v, window, dil, rel_bias):
    B, S, H, Dh = q.shape
    blk = window // dil
    L = -(-S // dil)
    nb = -(-L // blk)
    Lp = nb * blk

    def to_sub(t):
        t = jnp.pad(t, ((0, 0), (0, L * dil - S), (0, 0), (0, 0)))
        t = t.reshape(B, L, dil, H, Dh).transpose(0, 2, 1, 3, 4)
        return jnp.pad(t, ((0, 0), (0, 0), (0, Lp - L), (0, 0), (0, 0)))

    def band(t):
        t = jnp.pad(t, ((0, 0), (0, 0), (blk, 0), (0, 0), (0, 0))).reshape(B, dil, nb + 1, blk, H, Dh)
        return jnp.concatenate([t[:, :, :-1], t[:, :, 1:]], axis=3)

    qb = to_sub(q).reshape(B, dil, nb, blk, H, Dh)
    kb = band(to_sub(k))
    vb = band(to_sub(v))
    qi = jnp.arange(blk)[:, None]
    kj = jnp.arange(2 * blk)[None, :]
    delta = blk + qi - kj
    inwin = (delta >= 0) & (delta <= blk)
    bias = jnp.transpose(rel_bias[rel_bucket(jnp.clip(delta, 0, blk) * dil)], (2, 0, 1))
    blk_ok = (jnp.arange(nb)[:, None, None] > 0) | (jnp.arange(2 * blk) >= blk)[None, None, :]
    valid = inwin[None] & blk_ok
    s = jnp.einsum('bdnqhe,bdnkhe->bdnhqk', qb, kb).astype(jnp.float32) * (Dh ** -0.5) + bias
    s = jnp.where(valid[:, None], s, NEG)
    m = jnp.max(s, axis=-1)
    p = jnp.exp(s - m[..., None])
    l = jnp.sum(p, axis=-1)
    acc = jnp.einsum('bdnhqk,bdnkhe->bdnqhe', p, vb.astype(jnp.float32))

    def from_sub(t):
        tail = t.shape[5:]
        t = t.reshape((B, dil, Lp, H) + tail)[:, :, :L]
        t = jnp.moveaxis(t, 1, 2).reshape((B, L * dil, H) + tail)
        return t[:, :S]

    return from_sub(jnp.swapaxes(m, 3, 4)), from_sub(jnp.swapaxes(l, 3, 4)), from_sub(acc)


def dilated_branch_sample(q, kk, vv, past, window, dil, rel_bias):
    T, Dh = q.shape[1], q.shape[3]
    j = jnp.arange(window // dil + 1)
    idx = (past + jnp.arange(T))[:, None] - dil * j[None, :]
    valid = idx >= 0
    idx = jnp.maximum(idx, 0)
    kg = kk[:, idx]
    vg = vv[:, idx]
    bias = jnp.transpose(rel_bias[rel_bucket(j * dil)])
    s = jnp.einsum('bthe,btjhe->bthj', q, kg).astype(jnp.float32) * (Dh ** -0.5) + bias
    s = jnp.where(valid[None, :, None, :], s, NEG)
    m = jnp.max(s, axis=-1)
    p = jnp.exp(s - m[..., None])
    l = jnp.sum(p, axis=-1)
    acc = jnp.einsum('bthj,btjhe->bthe', p, vg.astype(jnp.float32))
    return m, l, acc


def merge_branches(parts):
    M = parts[0][0]
    for m, _, _ in parts[1:]:
        M = jnp.maximum(M, m)
    num = sum(jnp.exp(m - M)[..., None] * acc for m, _, acc in parts)
    den = sum(jnp.exp(m - M) * l for m, l, _ in parts)
    return num / den[..., None]


def gated_delta_rule(q, k, v, g, beta, S0):
    B, T, H, _ = q.shape
    pad = (-T) % CHUNK
    n = (T + pad) // CHUNK

    def chunks(t):
        t = jnp.pad(t.astype(jnp.float32), ((0, 0), (0, pad)) + ((0, 0),) * (t.ndim - 2))
        t = t.reshape((B, n, CHUNK) + t.shape[2:])
        return jnp.moveaxis(t, (1, 3), (0, 2))

    qc = chunks(q) * (DK ** -0.5)
    kc = chunks(k)
    vc = chunks(v)
    bc = chunks(beta)
    gc = jnp.cumsum(chunks(g), axis=-1)
    tri = jnp.tril(jnp.ones((CHUNK, CHUNK), bool))
    strict = jnp.tril(jnp.ones((CHUNK, CHUNK), bool), -1)
    decay = jnp.exp(jnp.where(tri, gc[..., :, None] - gc[..., None, :], NEG))
    kbeta = kc * bc[..., None]
    A = jnp.where(strict, jnp.einsum('nbhid,nbhjd->nbhij', kbeta, kc) * decay, 0.0)
    eye = jnp.eye(CHUNK, dtype=jnp.float32)
    Tm = lax.linalg.triangular_solve(eye + A, jnp.broadcast_to(eye, A.shape), left_side=True,
                                     lower=True, unit_diagonal=True)
    w = jnp.einsum('nbhij,nbhjd->nbhid', Tm, kbeta * jnp.exp(gc)[..., None])
    u = jnp.einsum('nbhij,nbhjd->nbhid', Tm, vc * bc[..., None])
    attn = jnp.einsum('nbhid,nbhjd->nbhij', qc, kc) * decay
    qg = qc * jnp.exp(gc)[..., None]
    kd = kc * jnp.exp(gc[..., -1:] - gc)[..., None]
    glast = jnp.exp(gc[..., -1])

    def step(S, xs):
        w_i, u_i, qg_i, attn_i, kd_i, gl_i = xs
        v_new = u_i - w_i @ S
        o = qg_i @ S + attn_i @ v_new
        S = S * gl_i[..., None, None] + jnp.swapaxes(kd_i, -1, -2) @ v_new
        return S, o

    S_T, o = lax.scan(step, S0.astype(jnp.float32), (w, u, qg, attn, kd, glast))
    o = jnp.moveaxis(o, (0, 2), (1, 3)).reshape(B, n * CHUNK, H, DV)[:, :T]
    return o, S_T


def delta_mixer(bq, bk, bv, bz, bb, ba, conv_buf, S0, dn_conv_w, dn_A_log, dn_dt_bias, dn_norm_w):
    B, T, _ = bq.shape
    qkv, conv_new = causal_dwconv(jnp.concatenate([bq, bk, bv], axis=-1), conv_buf, dn_conv_w)
    qkv = jax.nn.silu(qkv)
    q, k, v = jnp.split(qkv, (WIDTH_BQK, 2 * WIDTH_BQK), axis=-1)
    rep = V_HEADS_B // QK_HEADS_B
    q = jnp.repeat(l2norm(q.reshape(B, T, QK_HEADS_B, DK)), rep, axis=2)
    k = jnp.repeat(l2norm(k.reshape(B, T, QK_HEADS_B, DK)), rep, axis=2)
    v = v.reshape(B, T, V_HEADS_B, DV)
    beta = jax.nn.sigmoid(bb.astype(jnp.float32))
    g = -jnp.exp(dn_A_log.astype(jnp.float32)) * jax.nn.softplus(ba.astype(jnp.float32) + dn_dt_bias.astype(jnp.float32))
    o, S_new = gated_delta_rule(q, k, v, g, beta, S0)
    z = bz.reshape(B, T, V_HEADS_B, DV).astype(jnp.float32)
    o = o * lax.rsqrt(jnp.mean(o * o, axis=-1, keepdims=True) + EPS) * dn_norm_w.astype(jnp.float32) * jax.nn.silu(z)
    return o.reshape(B, T, WIDTH_BV).astype(bq.dtype), conv_new, S_new


def trunk_layer(x, win_k, win_v, dn_conv_buf, dn_state, ffn_buf, rel_bias,
                ln_mix_pre, w_in, dn_conv_w, dn_A_log, dn_dt_bias, dn_norm_w, w_out, ln_mix_post,
                ln_ffn_pre, w_ffn_in, ffn_conv_w, ffn_conv_b, w_ffn_out, ln_ffn_post):
    B, T, _ = x.shape
    h = rmsnorm(x, ln_mix_pre)
    proj = h @ w_in
    aq, ak, av, bq, bk, bv, bz, bb, ba = jnp.split(proj, SPLIT_IDX, axis=-1)
    aq = aq.reshape(B, T, HEADS_A, HEAD_DIM)
    ak = ak.reshape(B, T, HEADS_A, HEAD_DIM)
    av = av.reshape(B, T, HEADS_A, HEAD_DIM)
    if win_k is None:
        parts = [dilated_branch_prompt(aq, ak, av, wnd, dil, rel_bias) for wnd, dil in DILATED]
        kk, vv = ak, av
    else:
        past = win_k.shape[1]
        kk = jnp.concatenate([win_k.astype(ak.dtype), ak], axis=1)
        vv = jnp.concatenate([win_v.astype(av.dtype), av], axis=1)
        parts = [dilated_branch_sample(aq, kk, vv, past, wnd, dil, rel_bias) for wnd, dil in DILATED]
    keep = min(WIN_MAX, kk.shape[1])
    new_wk = kk[:, kk.shape[1] - keep:]
    new_wv = vv[:, vv.shape[1] - keep:]
    att = merge_branches(parts).reshape(B, T, WIDTH_A).astype(x.dtype)
    dn_out, conv_new, S_new = delta_mixer(bq, bk, bv, bz, bb, ba, dn_conv_buf, dn_state,
                                          dn_conv_w, dn_A_log, dn_dt_bias, dn_norm_w)
    mix = jnp.concatenate([att, dn_out], axis=-1) @ w_out
    x = x + rmsnorm(mix, ln_mix_post)
    h = rmsnorm(x, ln_ffn_pre)
    up, ffn_new = causal_dwconv(h @ w_ffn_in, ffn_buf, ffn_conv_w)
    gate, val = jnp.split(up + ffn_conv_b, 2, axis=-1)
    f = (jax.nn.gelu(gate, approximate=True) * val) @ w_ffn_out
    x = x + rmsnorm(f, ln_ffn_post)
    return x, new_wk, new_wv, conv_new, S_new, ffn_new


def setup_inputs(seed: int = 0) -> dict:
    key = jax.random.key(seed)
    ks = jax.random.split(key, 24)
    f32 = jnp.float32

    def nrm(k, shape, scale):
        return jax.random.normal(k, shape, f32) * scale

    win_buf = min(WIN_MAX, PAST_LEN)
    dt = jnp.exp(jax.random.uniform(ks[12], (DEPTH, V_HEADS_B), f32, math.log(1e-3), math.log(1e-1)))
    return {
        "x_prompt": nrm(ks[0], (BATCH, SEQ, D_MODEL), 1.0),
        "x_sample": nrm(ks[1], (DEC_BATCH, DEC_SEQ, D_MODEL), 1.0),
        "cache_win_k": nrm(ks[2], (DEPTH, DEC_BATCH, win_buf, HEADS_A, HEAD_DIM), 1.0),
        "cache_win_v": nrm(ks[3], (DEPTH, DEC_BATCH, win_buf, HEADS_A, HEAD_DIM), 1.0),
        "state_dn_conv": nrm(ks[4], (DEPTH, DEC_BATCH, CONV_W - 1, CONV_DIM), 1.0),
        "state_dn_rec": nrm(ks[5], (DEPTH, DEC_BATCH, V_HEADS_B, DK, DV), 0.1),
        "state_ffn_conv": nrm(ks[6], (DEPTH, DEC_BATCH, FFN_CONV_W - 1, 2 * D_FF), 1.0),
        "rel_bias": nrm(ks[7], (N_BUCKETS, HEADS_A), 0.5),
        "ln_mix_pre": 1.0 + nrm(ks[8], (DEPTH, D_MODEL), 0.05),
        "w_in": nrm(ks[9], (DEPTH, D_MODEL, PROJ_DIM), D_MODEL ** -0.5),
        "dn_conv_w": nrm(ks[10], (DEPTH, CONV_W, CONV_DIM), CONV_W ** -0.5),
        "dn_A_log": jnp.log(jax.random.uniform(ks[11], (DEPTH, V_HEADS_B), f32, 1.0, 16.0)),
        "dn_dt_bias": dt + jnp.log(-jnp.expm1(-dt)),
        "dn_norm_w": 1.0 + nrm(ks[13], (DEPTH, DV), 0.05),
        "w_out": nrm(ks[14], (DEPTH, MIX_OUT, D_MODEL), MIX_OUT ** -0.5),
        "ln_mix_post": 1.0 + nrm(ks[15], (DEPTH, D_MODEL), 0.05),
        "ln_ffn_pre": 1.0 + nrm(ks[16], (DEPTH, D_MODEL), 0.05),
        "w_ffn_in": nrm(ks[17], (DEPTH, D_MODEL, 2 * D_FF), D_MODEL ** -0.5),
        "ffn_conv_w": nrm(ks[18], (DEPTH, FFN_CONV_W, 2 * D_FF), FFN_CONV_W ** -0.5),
        "ffn_conv_b": nrm(ks[19], (DEPTH, 2 * D_FF), 0.02),
        "w_ffn_out": nrm(ks[20], (DEPTH, D_FF, D_MODEL), D_FF ** -0.5),
        "ln_ffn_post": 1.0 + nrm(ks[21], (DEPTH, D_MODEL), 0.05),
    }


def reference(x_prompt, x_sample, cache_win_k, cache_win_v, state_dn_conv, state_dn_rec, state_ffn_conv,
              rel_bias, ln_mix_pre, w_in, dn_conv_w, dn_A_log, dn_dt_bias, dn_norm_w, w_out, ln_mix_post,
              ln_ffn_pre, w_ffn_in, ffn_conv_w, ffn_conv_b, w_ffn_out, ln_ffn_post):
    yp, ys = x_prompt, x_sample
    bp = x_prompt.shape[0]
    p_wk, p_wv, p_dc, p_dr, p_fc = [], [], [], [], []
    s_wk, s_wv, s_dc, s_dr, s_fc = [], [], [], [], []
    for l in range(DEPTH):
        weights = (rel_bias, ln_mix_pre[l], w_in[l], dn_conv_w[l], dn_A_log[l], dn_dt_bias[l], dn_norm_w[l],
                   w_out[l], ln_mix_post[l], ln_ffn_pre[l], w_ffn_in[l], ffn_conv_w[l], ffn_conv_b[l],
                   w_ffn_out[l], ln_ffn_post[l])
        yp, wk, wv, dc, dr, fc = trunk_layer(
            yp, None, None,
            jnp.zeros((bp, CONV_W - 1, CONV_DIM), yp.dtype),
            jnp.zeros((bp, V_HEADS_B, DK, DV), jnp.float32),
            jnp.zeros((bp, FFN_CONV_W - 1, 2 * D_FF), yp.dtype),
            *weights)
        p_wk.append(wk); p_wv.append(wv); p_dc.append(dc); p_dr.append(dr); p_fc.append(fc)
        ys, wk, wv, dc, dr, fc = trunk_layer(
            ys, cache_win_k[l], cache_win_v[l], state_dn_conv[l], state_dn_rec[l], state_ffn_conv[l],
            *weights)
        s_wk.append(wk); s_wv.append(wv); s_dc.append(dc); s_dr.append(dr); s_fc.append(fc)
    return (yp, ys,
            jnp.stack(p_wk), jnp.stack(p_wv), jnp.stack(p_dc), jnp.stack(p_dr), jnp.stack(p_fc),
            jnp.stack(s_wk), jnp.stack(s_wv), jnp.stack(s_dc), jnp.stack(s_dr), jnp.stack(s_fc))
```

```python
import math
import itertools
from contextlib import ExitStack

import numpy as np
import concourse.bass as bass
import concourse.mybir as mybir
from concourse.bass_utils import run_bass_kernel_spmd

F32 = mybir.dt.float32
BF16 = mybir.dt.bfloat16
ALU = mybir.AluOpType
AF = mybir.ActivationFunctionType
AX = mybir.AxisListType

NCORES = 8
D = 2048
KC = 16
VT = 4096
NT = VT + 128
NTILES = NT // 128
P2_0 = 2944
P2N = NT - P2_0
PROJ = 6160
DFF = 5632
EPS = 1e-6
NEG = -1e30
DILS = (1, 4, 16)
ENGS = ("pe", "act", "dve", "pool", "sp")


class Prog:
    NDMA = 10

    def __init__(self, nc, es):
        self.nc = nc
        self.streams = {e: [] for e in ENGS}
        self.count = {e: 0 for e in ENGS}
        self.sem = {e: es.enter_context(nc.semaphore("s_" + e)) for e in ("pe", "act", "dve", "pool")}
        self.dsem, self.dcnt, self.dlast, self.dnum = {}, {}, {}, {}
        for q in ("sp", "pool", "act"):
            self.dsem[q] = [es.enter_context(nc.semaphore(f"d_{q}{i}")) for i in range(self.NDMA)]
            self.dcnt[q] = [0] * self.NDMA
            self.dlast[q] = [None] * self.NDMA
            self.dnum[q] = 0
        self.last_w, self.readers = {}, {}
        self.seen = {e: {} for e in ENGS}
        self.fence_toks = []
        self.marks = []

    def _tw(self, t):
        if t[0] == "c":
            return self.sem[t[1]], t[2], "c_" + t[1]
        return self.dsem[t[1]][t[2]], t[3], f"d_{t[1]}{t[2]}"

    def _deps(self, eng, reads, writes, extra=()):
        toks = list(extra) + self.fence_toks
        for r in reads:
            t = self.last_w.get(r)
            if t is not None:
                toks.append(t)
        for w in writes:
            t = self.last_w.get(w)
            if t is not None:
                toks.append(t)
            toks.extend(self.readers.get(w, ()))
        waits = {}
        for t in toks:
            if t[0] == "c" and t[1] == "pe" and eng == "pe":
                continue
            s, v, name = self._tw(t)
            if self.seen[eng].get(name, 0) >= v:
                continue
            if name not in waits or waits[name][1] < v:
                waits[name] = (s, v)
        for name, (s, v) in waits.items():
            self.seen[eng][name] = v
        return list(waits.values())

    def _commit(self, tok, reads, writes):
        for r in reads:
            self.readers.setdefault(r, []).append(tok)
        for w in writes:
            self.last_w[w] = tok
            self.readers[w] = []

    def op(self, eng, fn, reads=(), writes=()):
        waits = self._deps(eng, reads, writes)
        self.count[eng] += 1
        tok = ("c", eng, self.count[eng])
        self.streams[eng].append((waits, fn, (self.sem[eng], 1)))
        self._commit(tok, reads, writes)

    def dma(self, q, fn, reads=(), writes=()):
        i = self.dnum[q] % self.NDMA
        self.dnum[q] += 1
        extra = [self.dlast[q][i]] if self.dlast[q][i] is not None else []
        waits = self._deps(q, reads, writes, extra)
        self.dcnt[q][i] += 16
        tok = ("d", q, i, self.dcnt[q][i])
        self.dlast[q][i] = tok
        self.streams[q].append((waits, fn, (self.dsem[q][i], 16)))
        self._commit(tok, reads, writes)

    def mark(self, label):
        self.marks.append((label, dict(self.count)))

    def fence(self):
        toks = []
        for e in ("pe", "act", "dve", "pool"):
            if self.count[e]:
                toks.append(("c", e, self.count[e]))
        for q in self.dsem:
            for i in range(self.NDMA):
                if self.dcnt[q][i]:
                    toks.append(("d", q, i, self.dcnt[q][i]))
        self.fence_toks = toks

    def finish(self):
        nc = self.nc
        final = []
        for q in self.dsem:
            for i in range(self.NDMA):
                if self.dcnt[q][i]:
                    final.append((self.dsem[q][i], self.dcnt[q][i]))
        for e in ("pe", "act", "dve", "pool"):
            if self.count[e]:
                final.append((self.sem[e], self.count[e]))
        streams = self.streams

        def emit(eng, lst, tail=()):
            for waits, fn, inc in lst:
                for s, v in waits:
                    eng.wait_ge(s, v)
                fn(eng).then_inc(inc[0], inc[1])
            for s, v in tail:
                eng.wait_ge(s, v)

        with nc.Block() as block:
            @block.sync
            def _(e):
                emit(e, streams["sp"], final)

            @block.tensor
            def _(e):
                emit(e, streams["pe"])

            @block.scalar
            def _(e):
                emit(e, streams["act"])

            @block.vector
            def _(e):
                emit(e, streams["dve"])

            @block.gpsimd
            def _(e):
                emit(e, streams["pool"])


def rel_bucket_np(dist):
    max_exact = 16
    d = np.maximum(dist, 1).astype(np.float32)
    large = max_exact + (np.log(d / max_exact) / np.float32(math.log(2048 / max_exact)) * (32 - max_exact)).astype(np.int32)
    large = np.minimum(large, 31)
    return np.where(dist < max_exact, dist, large)


def host_consts():
    k = np.arange(128)[:, None]
    i = np.arange(128)[None, :]
    same = (k // 64) == (i // 64)
    c = {}
    c["ident"] = np.eye(128, dtype=np.float32)
    c["ones"] = np.ones((128, 128), np.float32)
    c["U"] = (same & (k <= i)).astype(np.float32)
    c["UST"] = (same & (k > i)).astype(np.float32)
    c["MT"] = np.where(same & (i >= k), 0.0, NEG).astype(np.float32)
    c["SU"] = (same & (i > k)).astype(np.float32)
    cm = np.concatenate([c[n] for n in ("ident", "ones", "U", "UST", "MT", "SU")], axis=1)
    ch = np.stack([(np.arange(128) < 64), (np.arange(128) >= 64)], 1).astype(np.float32)
    oh = np.zeros((32, 3, 384), np.float32)
    negv = np.full((3, 384), NEG, np.float32)
    for b, dil in enumerate(DILS):
        dl = np.arange(129)
        bk = rel_bucket_np(dl * dil)
        oh[bk, b, 128 + dl] = 1.0
        negv[b, 128:257] = 0.0
    return cm, ch, oh.reshape(32, 3 * 384), negv.reshape(1, 3 * 384)


def att_units():
    ranges = []
    for m in range(3):
        base = 2560 + 512 * m
        units = []
        for n in range(20 + 4 * m, 24 + 4 * m):
            units.append(dict(b=0, dil=1, r=0, n=n, g=None, q0=128 * n, nq=128, ocol=128 * n - base, ostep=1, msl=(0, 128)))
        for r in range(4):
            n = 5 + m
            units.append(dict(b=1, dil=4, r=r, n=n, g=None, q0=4 * 128 * n + r, nq=128, ocol=r, ostep=4, msl=(0, 128)))
        for r in range(16):
            g = 1 + m
            units.append(dict(b=2, dil=16, r=r, n=1, g=g, q0=16 * (128 + 32 * g) + r, nq=32, ocol=r, ostep=16, msl=(32 * g, 32 * g + 32)))
        ranges.append((base, units))
    return ranges


def key_blocks():
    kb = {}
    for n in range(19, 32):
        kb[(0, 0, n)] = len(kb)
    for r in range(4):
        for n in range(4, 8):
            kb[(1, r, n)] = len(kb)
    for r in range(16):
        for n in range(2):
            kb[(2, r, n)] = len(kb)
    return kb


def build(debug=False, stop_after=None):
    nc = bass.Bass("TRN2", target_bir_lowering=False)
    din = lambda name, shape, dt=F32: nc.dram_tensor(name, list(shape), dt, kind="ExternalInput").ap()
    dout = lambda name, shape, dt=F32: nc.dram_tensor(name, list(shape), dt, kind="ExternalOutput").ap()
    dscr = lambda name, shape, dt=F32: nc.dram_tensor(name, list(shape), dt, kind="Internal").ap()

    xT = din("xT", [D, NT])
    kmb = din("kmb", [128, 96])
    cm_d = din("cm", [128, 768])
    oh_d = din("oh", [32, 3 * 384])
    negv_d = din("negv", [1, 3 * 384])
    rel_bias = din("rel_bias", [32, 8])
    ln1 = din("ln1", [128, KC])
    w_in = din("w_in", [49, 128, KC, 128])
    cache_k = din("cache_k", [2048, 1024])
    cache_v = din("cache_v", [2048, 1024])
    st_dnc = din("st_dnc", [3, 2048])
    cw_d = din("cw", [128, 16 * 4])
    stf_d = din("stf", [128, 16 * 3])
    alog_d = din("alog", [1, 8])
    dtb_d = din("dtb", [1, 8])
    nw_d = din("nw", [1, 128])
    gmask_d = din("gmask", [128, NTILES])
    chm_d = din("chm", [128, 2])
    st_rec = din("st_rec", [8, 128, 128])

    o_wk = dout("o_wk", [8, 128, 2048])
    o_wv = dout("o_wv", [8, 128, 2048])
    o_swk = dout("o_swk", [2048, 1024])
    o_swv = dout("o_swv", [2048, 1024])
    o_dnc = dout("o_dnc", [16, 128, 4])
    o_sdc = dout("o_sdc", [3, 2048])
    o_pdr = dout("o_pdr", [8, 128, 128])
    o_sdr = dout("o_sdr", [8, 128, 128])
    w_out = din("w_out", [KC, 128, KC, 128])
    w_ffn_in = din("w_ffn_in", [2 * DFF // 128, 128, KC, 128])
    w_ffn_out = din("w_ffn_out", [KC, 128, DFF // 128, 128])
    lnp_d = din("lnp", [128, 3 * KC])
    fcw_d = din("fcw", [128, 88 * 3])
    fcb_d = din("fcb", [128, 88])
    fst_d = din("fst", [128, 88 * 2])
    st_ffc = din("st_ffc", [2, 2 * DFF])
    o_y = dout("o_y", [D, P2N - 255])
    o_pfc = dout("o_pfc", [2, DFF // 128, 128, 2])
    o_sfc = dout("o_sfc", [2, 2 * DFF])
    x1D = dscr("x1D", [D, P2N])
    o_mix = dout("o_mix", [D, P2N]) if debug else None

    mixD = dscr("mixD", [D, P2N], BF16)
    Yscr = dscr("Yscr", [128, 384])
    rowD = dscr("rowD", [3, 1024])

    KB = key_blocks()
    RANGES = att_units()

    with ExitStack() as es:
        P = Prog(nc, es)
        sb = lambda name, shape, dt=F32: es.enter_context(nc.sbuf_tensor(name, list(shape), dt))
        psb = [es.enter_context(nc.psum_tensor(f"ps{i}", [128, 512], F32)) for i in range(8)]

        def MM(out, lhsT, rhs, start=True, stop=True, R=(), W=(), skip=False):
            P.op("pe", lambda e: e.matmul(out, lhsT=lhsT, rhs=rhs, start=start, stop=stop, skip_group_check=skip), reads=R, writes=W)

        def TRP(out, in_, idn, R=(), W=()):
            P.op("pe", lambda e: e.transpose(out, in_, idn), reads=R, writes=W)

        def ACTF(out, in_, func, R=(), W=(), bias=None, scale=1.0, accum=None):
            kw = {}
            if bias is not None:
                kw["bias"] = bias
            if accum is not None:
                kw["accum_out"] = accum
            P.op("act", lambda e: e.activation(out, in_, func, scale=scale, **kw), reads=R, writes=W)

        def CP(out, in_, R=(), W=(), eng="dve"):
            if eng == "act":
                ACTF(out, in_, AF.Copy, R, W)
            else:
                P.op(eng, lambda e: e.tensor_copy(out, in_), reads=R, writes=W)

        def TT(out, in0, in1, op, R=(), W=(), eng="dve"):
            P.op(eng, lambda e: e.tensor_tensor(out, in0, in1, op), reads=R, writes=W)

        def TS(out, in0, s1, s2, op0, op1=None, R=(), W=(), eng="dve"):
            if op1 is None:
                P.op(eng, lambda e: e.tensor_scalar(out, in0, s1, s2, op0), reads=R, writes=W)
            else:
                P.op(eng, lambda e: e.tensor_scalar(out, in0, s1, s2, op0, op1), reads=R, writes=W)

        def STT(out, in0, scalar, in1, op0, op1, R=(), W=()):
            P.op("dve", lambda e: e.scalar_tensor_tensor(out=out, in0=in0, scalar=scalar, in1=in1, op0=op0, op1=op1), reads=R, writes=W)

        def RCP(out, in_, R=(), W=()):
            P.op("dve", lambda e: e.reciprocal(out, in_), reads=R, writes=W)

        def MSET(ap, val, W=(), eng="dve"):
            P.op(eng, lambda e: e.memset(ap, val), writes=W)

        def DMA(q, out, in_, R=(), W=(), slow=False):
            if slow:
                P.dma(q, lambda e: e.dma_start(out=out, in_=in_, allow_slow_non_contiguous=True), reads=R, writes=W)
            else:
                P.dma(q, lambda e: e.dma_start(out=out, in_=in_), reads=R, writes=W)

        cm = sb("cm_s", [128, 768])
        DMA("sp", cm[:], cm_d, W=["cm"])
        ident = cm[:, 0:128]
        ones32 = cm[:, 128:256]
        Um = cm[:, 256:384]
        USTm = cm[:, 384:512]
        MTm = cm[:, 512:640]
        SUm = cm[:, 640:768]
        ones_bf = sb("ones_bf", [128, 128], BF16)
        ident_bf = sb("ident_bf", [128, 128], BF16)
        CP(ones_bf[:], cm[:, 128:256], R=["cm"], W=["ones_bf"])
        CP(ident_bf[:], cm[:, 0:128], R=["cm"], W=["ident_bf"])
        ln1_s = sb("ln1_s", [128, KC])
        DMA("sp", ln1_s[:], ln1, W=["ln1"])
        kmb_s = sb("kmb_s", [128, 96])
        DMA("sp", kmb_s[:], kmb, W=["kmb"])
        eps_t = sb("eps_t", [128, 1])
        one_t = sb("one_t", [128, 1])
        MSET(eps_t[:], EPS, W=["eps"])
        MSET(one_t[:], 1.0, W=["one"])

        esX = ExitStack()
        sbx = lambda name, shape, dt=F32: esX.enter_context(nc.sbuf_tensor(name, list(shape), dt))
        xbf = sbx("xbf", [128, KC, NT], BF16)
        blocks = [(512 * i, 512) for i in range(8)] + [(VT, 128)]
        with ExitStack() as es0:
            stage = es0.enter_context(nc.sbuf_tensor("stage", [128, KC, 512], F32))
            sq = es0.enter_context(nc.sbuf_tensor("sq", [128, 2, 512], BF16))
            rb = es0.enter_context(nc.sbuf_tensor("rb", [128, 512], F32))
            for bi, (c0, n) in enumerate(blocks):
                for kc in range(KC):
                    DMA("sp" if kc % 2 == 0 else "act", stage[:, kc, 0:n], xT[kc * 128:(kc + 1) * 128, c0:c0 + n], W=[f"stage{kc}"])
                for kc in range(KC):
                    ACTF(sq[:, kc % 2, 0:n], stage[:, kc, 0:n], AF.Square, R=[f"stage{kc}"], W=[f"sq{kc % 2}"])
                    MM(psb[0][:, 0:n], ones_bf[:], sq[:, kc % 2, 0:n], start=(kc == 0), stop=(kc == KC - 1), R=[f"sq{kc % 2}", "ones_bf"], W=["ps0"])
                ACTF(rb[:, 0:n], psb[0][:, 0:n], AF.Sqrt, R=["eps"], W=["ps0", "rb"], bias=eps_t[:], scale=1.0 / D)
                RCP(rb[:, 0:n], rb[:, 0:n], W=["rb"])
                for kc in range(KC):
                    STT(xbf[:, kc, c0:c0 + n], stage[:, kc, 0:n], ln1_s[:, kc:kc + 1], rb[:, 0:n], ALU.mult, ALU.mult,
                        R=[f"stage{kc}", "rb", "ln1"], W=[f"xbf{bi}"])
        P.mark('phase0_done')
        P.fence()
        xbf_all = [f"xbf{bi}" for bi in range(len(blocks))]

        wt = [sbx(f"wt{i}", [128, KC, 128], BF16) for i in range(2)]
        wt_n = [0]

        def load_w(c0):
            i = wt_n[0] % 2
            wt_n[0] += 1
            DMA("pool", wt[i][:], w_in[c0 // 128], W=[f"wt{i}"])
            return i

        def gemm_fm(wi, t0, n, ps, pname):
            for kc in range(KC):
                MM(ps[:, 0:n], wt[wi][:, kc, :], xbf[:, kc, t0:t0 + n], start=(kc == 0), stop=(kc == KC - 1), R=[f"wt{wi}"] + xbf_all, W=[pname])

        sbias = sbx("sbias", [128, 3, 8])
        sself = sbx("sself", [1, 3, 8])
        with ExitStack() as esA:
            sba = lambda name, shape, dt=F32: esA.enter_context(nc.sbuf_tensor(name, list(shape), dt))
            rowt = sba("rowt", [1, 128])
            KT = sba("KT", [128, VT], BF16)
            VTt = sba("VTt", [128, VT], BF16)
            QT = sba("QT", [128, 2048], BF16)
            Vtm = sba("Vtm", [128, len(KB), 128], BF16)
            kvf = sba("kvf", [128, 2, 512], F32)
            Bm = sba("Bm", [128, 6, 128], F32)
            Erow = sba("Erow", [128, 384], F32)
            oh_s = sba("oh_s", [32, 3 * 384])
            negv_s = sba("negv_s", [128, 3 * 384])
            rbias_s = sba("rbias_s", [32, 8])
            rbb = sba("rbb", [32, 128])
            DMA("sp", oh_s[:], oh_d, W=["oh"])
            DMA("sp", negv_s[:], negv_d.partition_broadcast(128), W=["negv"])
            DMA("sp", rbias_s[:], rel_bias, W=["rbias"])
            ssc = sba("ssc", [128, 8, 128], F32)
            PT = [sba(f"PT{i}", [128, 2, 128], BF16) for i in range(4)]
            attb = sba("attb", [128, 512], BF16)
            dent = sba("dent", [128, 512], F32)
            scale = 128 ** -0.5
            unit_n = 0
            for h in range(8 if stop_after != "skipatt" else 0):
                CP(rbb[:], rbias_s[:, h:h + 1].to_broadcast([32, 128]), R=["rbias"], W=["rbb"])
                for b in range(3):
                    MM(psb[1][:, 0:384], rbb[:], oh_s[:, b * 384:(b + 1) * 384], R=["rbb", "oh"], W=["ps1"])
                    TT(Erow[:], psb[1][:, 0:384], negv_s[:, b * 384:(b + 1) * 384], ALU.add, R=["negv"], W=["ps1", "Erow"])
                    DMA("sp", Yscr, Erow[:], R=["Erow"], W=["Yscr"])
                    for cp, base in ((0, 128), (1, 256)):
                        DMA("sp", Bm[:, 2 * b + cp, :], bass.AP(Yscr.tensor, base, [[383, 128], [1, 128]]), R=["Yscr"], W=["Bm"])
                for b in range(3):
                    CP(sbias[:, b, h:h + 1], Bm[:, 2 * b + 1, 0:1], R=["Bm"], W=["sbias"])
                    CP(sself[0:1, b, h:h + 1], Bm[0:1, 2 * b, 0:1], R=["Bm"], W=["sself"])
                for which, c0, dst, t_lo in (("k", 1024 + 128 * h, KT, 0), ("v", 2048 + 128 * h, VTt, 0), ("q", 128 * h, QT, 2048)):
                    t_first = t_lo // 512 if which != "q" else 5
                    wi = load_w(c0)
                    for tb in range(t_first, 8):
                        pb = 2 + (tb % 2)
                        gemm_fm(wi, 512 * tb, 512, psb[pb], f"ps{pb}")
                        dsl = dst[:, 512 * tb - t_lo:512 * tb - t_lo + 512]
                        if which != "q" and tb >= 4:
                            kb_ = tb % 2
                            CP(kvf[:, kb_, :], psb[pb][:, :], W=[f"ps{pb}", f"kvf{kb_}"])
                            CP(dsl, kvf[:, kb_, :], R=[f"kvf{kb_}"], W=[which + "T"], eng="act")
                            od = o_wk if which == "k" else o_wv
                            DMA("sp", od[h, :, 512 * (tb - 4):512 * (tb - 4) + 512], kvf[:, kb_, :], R=[f"kvf{kb_}"], W=["o_w"])
                        else:
                            CP(dsl, psb[pb][:, :], W=[f"ps{pb}", which + "T"], eng="act")
                    for kc in range(KC):
                        MM(psb[0][0:1, 0:128], xbf[:, kc, VT:VT + 1], wt[wi][:, kc, :], start=(kc == 0), stop=(kc == KC - 1), R=[f"wt{wi}"] + xbf_all, W=["ps0"])
                    CP(rowt[:], psb[0][0:1, 0:128], W=["ps0", "rowt"])
                    DMA("sp", rowD[{"q": 0, "k": 1, "v": 2}[which]:{"q": 0, "k": 1, "v": 2}[which] + 1, 128 * h:128 * h + 128], rowt[:], R=["rowt"], W=["rowD"])
                    if which != "q":
                        gemm_fm(wi, VT, 128, psb[2], "ps2")
                        CP(kvf[:, 0, 0:128], psb[2][:, 0:128], W=["ps2", "kvf0"])
                        od = o_swk if which == "k" else o_swv
                        DMA("sp", od[2047:2048, 128 * h:128 * h + 128].rearrange("a d -> d a"), kvf[:, 0, 0:1], R=["kvf0"], W=["o_sw"])
                for (b, r, n), idx in KB.items():
                    dil = DILS[b]
                    t0 = dil * 128 * n + r
                    pb = 4 + (idx % 2)
                    pv = psb[pb][:, :].bitcast(BF16)[:, 0:128]
                    TRP(pv, VTt[:, t0:t0 + 127 * dil + 1:dil], ident_bf[:], R=["vT", "ident_bf"], W=[f"ps{pb}"])
                    CP(Vtm[:, idx, :], pv, W=[f"ps{pb}", "Vtm"], eng=("act" if idx % 2 == 0 else "dve"))
                for m, (base, units) in enumerate(RANGES):
                    def stageA(u, ui):
                        b, dil, r, n, nq = u["b"], u["dil"], u["r"], u["n"], u["nq"]
                        sps = psb[2 + ui]
                        sn = f"ps{2 + ui}"
                        q0 = u["q0"] - 2048
                        qsl = slice(q0, q0 + (nq - 1) * dil + 1, dil)
                        for cp, kn in ((0, n), (1, n - 1)):
                            k0 = dil * 128 * kn + r
                            MM(sps[:, cp * 128:cp * 128 + nq], KT[:, k0:k0 + 127 * dil + 1:dil], QT[:, qsl], R=["kT", "qT"], W=[sn])
                        for cp in (0, 1):
                            STT(ssc[:, 2 * ui + cp, 0:nq], sps[:, cp * 128:cp * 128 + nq], scale, Bm[:, 2 * b + cp, u["msl"][0]:u["msl"][1]], ALU.mult, ALU.add,
                                R=["Bm"], W=[sn, f"ssc{ui}_{cp}"])
                        for cp, kn in ((0, n), (1, n - 1)):
                            kbi = {0: kn, 1: 32 + r * 8 + kn, 2: 64 + r * 2 + kn}[b]
                            ACTF(PT[ui][:, cp, 0:nq], ssc[:, 2 * ui + cp, 0:nq], AF.Exp, R=[f"ssc{ui}_{cp}", "kmb"], W=[f"PT{ui}_{cp}"], bias=kmb_s[:, kbi:kbi + 1])

                    def stageB(u, ui, first):
                        b, r, n, nq = u["b"], u["r"], u["n"], u["nq"]
                        osl = slice(u["ocol"], u["ocol"] + (nq - 1) * u["ostep"] + 1, u["ostep"])
                        for cp, kn in ((0, n), (1, n - 1)):
                            vi = KB[(b, r, kn)]
                            st = first and cp == 0
                            MM(psb[6][:, osl], Vtm[:, vi, :], PT[ui][:, cp, 0:nq], start=st, stop=False, R=["Vtm", f"PT{ui}_{cp}"], W=["ps6"], skip=True)
                            MM(psb[7][:, osl], ones_bf[:], PT[ui][:, cp, 0:nq], start=st, stop=False, R=["ones_bf", f"PT{ui}_{cp}"], W=["ps7"], skip=True)

                    DEPTH = 3
                    for i_, u in enumerate(units):
                        stageA(u, i_ % 4)
                        if i_ >= DEPTH:
                            stageB(units[i_ - DEPTH], (i_ - DEPTH) % 4, i_ - DEPTH == 0)
                    for i_ in range(max(len(units) - DEPTH, 0), len(units)):
                        stageB(units[i_], i_ % 4, i_ == 0)
                    TS(dent[:], psb[7][:, :], 1e-30, None, ALU.max, W=["ps7", "dent"])
                    RCP(dent[:], dent[:], W=["dent"])
                    TT(attb[:], psb[6][:, :], dent[:], ALU.mult, R=["dent"], W=["ps6", "attb"])
                    lo = max(base, P2_0)
                    DMA("sp", mixD[128 * h:128 * h + 128, lo - P2_0:base + 512 - P2_0], attb[:, lo - base:512], R=["attb"], W=["mixD"])
        P.fence()

        P.mark('att_done')
        SCOL = P2N - 128
        zt = sbx("zt", [128, 128], BF16)
        MSET(zt[:], 0.0, W=["zt"])
        for h in range(8):
            DMA("sp", mixD[128 * h:128 * h + 128, P2N - 128:P2N], zt[:], R=["zt"], W=["mixD"])

        with ExitStack() as esS:
            sbs = lambda name, shape, dt=F32: esS.enter_context(nc.sbuf_tensor(name, list(shape), dt))
            qkv = sbs("qkvrow", [1, 3072])
            qb = sbs("qb", [128, 1024])
            Kc = sbs("Kc", [128, 1024])
            Vc = sbs("Vc", [128, 1024])
            s8 = sbs("s8", [128, 8])
            p8 = sbs("p8", [128, 8])
            r8 = sbs("r8", [1, 8, 4])
            prow = sbs("prow", [1, 1024])
            arow = sbs("arow", [1, 1024])
            acol = sbs("acol", [128, 8], BF16)
            DMA("sp", qkv[:], rowD.rearrange("a (o c) -> o (a c)", o=1), R=["rowD"], W=["qkv"])
            qrow, krow, vrow = qkv[0:1, 0:1024], qkv[0:1, 1024:2048], qkv[0:1, 2048:3072]
            for hf in range(2):
                MM(psb[hf][:, 0:512], ones32[0:1, 0:128], qkv[0:1, 512 * hf:512 * hf + 512], R=["cm", "qkv"], W=[f"ps{hf}"])
                CP(qb[:, 512 * hf:512 * hf + 512], psb[hf][:, 0:512], W=[f"ps{hf}", "qb"])
            for b, dil in enumerate(DILS):
                off = (2048 - 128 * dil) * 1024
                DMA("sp", Kc[:], bass.AP(cache_k.tensor, off, [[dil * 1024, 128], [1, 1024]]), W=["Kc"])
                DMA("act", Vc[:], bass.AP(cache_v.tensor, off, [[dil * 1024, 128], [1, 1024]]), W=["Vc"])
                TT(Kc[:], Kc[:], qb[:], ALU.mult, R=["qb"], W=["Kc"])
                P.op("dve", lambda e: e.tensor_reduce(s8[:], Kc[:].rearrange("p (h d) -> p h d", h=8), AX.X, ALU.add), reads=["Kc"], writes=["s8"])
                STT(s8[:], s8[:], scale, sbias[:, b, :], ALU.mult, ALU.add, R=["sbias"], W=["s8"])
                ACTF(p8[:], s8[:], AF.Exp, R=["s8"], W=["p8"])
                TT(Vc[:].rearrange("p (h d) -> p h d", h=8), Vc[:].rearrange("p (h d) -> p h d", h=8), p8[:].unsqueeze(2).to_broadcast([128, 8, 128]), ALU.mult, R=["p8"], W=["Vc"])
                for hf in range(2):
                    MM(psb[2 + hf][0:1, 0:512], ones32[:, 0:1], Vc[:, 512 * hf:512 * hf + 512], start=(b == 0), stop=(b == 2), R=["cm", "Vc"], W=[f"ps{2 + hf}"])
                MM(psb[4][0:1, 0:8], ones32[:, 0:1], p8[:], start=(b == 0), stop=(b == 2), R=["cm", "p8"], W=["ps4"])
            TT(prow[:], qrow, krow, ALU.mult, R=["qkv"], W=["prow"])
            P.op("dve", lambda e: e.tensor_reduce(r8[0:1, :, 0], prow[:].rearrange("p (h d) -> p h d", h=8), AX.X, ALU.add), reads=["prow"], writes=["r8"])
            for b in range(3):
                STT(r8[0:1, :, 1 + b], r8[0:1, :, 0], scale, sself[0:1, b, :], ALU.mult, ALU.add, R=["sself"], W=["r8"])
            ACTF(r8[0:1, :, 1:4], r8[0:1, :, 1:4], AF.Exp, W=["r8"])
            P.op("dve", lambda e: e.tensor_reduce(r8[0:1, :, 0], r8[0:1, :, 1:4], AX.X, ALU.add), writes=["r8"])
            TT(prow[:].rearrange("p (h d) -> p h d", h=8), vrow.rearrange("p (h d) -> p h d", h=8), r8[0:1, :, 0:1].to_broadcast([1, 8, 128]), ALU.mult, R=["qkv", "r8"], W=["prow"])
            for hf in range(2):
                TT(arow[0:1, 512 * hf:512 * hf + 512], psb[2 + hf][0:1, 0:512], prow[0:1, 512 * hf:512 * hf + 512], ALU.add, R=["prow"], W=[f"ps{2 + hf}", "arow"])
            TT(r8[0:1, :, 1], psb[4][0:1, 0:8], r8[0:1, :, 0], ALU.add, W=["ps4", "r8"])
            RCP(r8[0:1, :, 1], r8[0:1, :, 1], W=["r8"])
            TT(arow[:].rearrange("p (h d) -> p h d", h=8), arow[:].rearrange("p (h d) -> p h d", h=8), r8[0:1, :, 1:2].to_broadcast([1, 8, 128]), ALU.mult, R=["r8"], W=["arow"])
            for h in range(8):
                MM(psb[5][:, h:h + 1], arow[0:1, 128 * h:128 * h + 128], ones32[0:1, 0:1], R=["arow", "cm"], W=["ps5"], skip=True)
            CP(acol[:], psb[5][:, 0:8], W=["ps5", "acol"])
            for h in range(8):
                DMA("sp", mixD[128 * h:128 * h + 128, SCOL:SCOL + 1], acol[:, h:h + 1], R=["acol"], W=["mixD"], slow=True)
        P.fence()
        P.mark('sample_att_done')
        dnc_s = sbx("dnc_s", [128, 4], F32)
        for cb in range(16):
            wi = load_w(3072 + 128 * cb)
            for kc in range(KC):
                MM(psb[2][:, 0:4], wt[wi][:, kc, :], xbf[:, kc, VT - 3:VT + 1], start=(kc == 0), stop=(kc == KC - 1), R=[f"wt{wi}"] + xbf_all, W=["ps2"])
            CP(dnc_s[:], psb[2][:, 0:4], W=["ps2", "dnc_s"])
            DMA("sp", o_dnc[cb], dnc_s[:], R=["dnc_s"], W=["o_dnc"])
            DMA("sp", o_sdc[2:3, 128 * cb:128 * cb + 128].rearrange("a d -> d a"), dnc_s[:, 3:4], R=["dnc_s"], W=["o_sdc"])

        copy_chunks = []
        for (src, dst) in ((cache_k, o_swk), (cache_v, o_swv)):
            for i in range(64):
                w_ = 256 if i < 63 else 16376 - 63 * 256
                copy_chunks.append((bass.AP(dst.tensor, 256 * i, [[16376, 128], [1, w_]]), bass.AP(src.tensor, 1024 + 256 * i, [[16376, 128], [1, w_]])))
        DMA("sp", o_sdc[0:2, :], st_dnc[1:3, :], W=["o_sdc"])
        DMA("sp", o_sfc[0:1, :], st_ffc[1:2, :], W=["o_sfc"])

        if stop_after != "att":
            with ExitStack() as esD:
                NG = NTILES * 8
                sbd = lambda name, shape, dt=F32: esD.enter_context(nc.sbuf_tensor(name, list(shape), dt))
                nw_s = sbd("nw_s", [128, 128])
                gmask_s = sbd("gmask_s", [128, NTILES])
                chm_s = sbd("chm_s", [128, 2])
                cw_s = sbd("cw_s", [128, 16, 4])
                stf_s = sbd("stf_s", [128, 16, 3])
                beta = sbd("beta", [128, NTILES, 8])
                g = sbd("g", [128, NTILES, 8])
                gcol = sbd("gcol", [128, NG])
                egc = sbd("egc", [128, NG])
                negegc = sbd("negegc", [128, NG])
                egrem = sbd("egrem", [128, NG])
                egsum = sbd("egsum", [128, 2 * NG])
                esT = ExitStack()
                sbt = lambda name, shape, dt=F32: esT.enter_context(nc.sbuf_tensor(name, list(shape), dt))
                gsel = sbt("gsel", [128, NTILES, 2, 8])
                wba = sbt("wba", [128, KC, 16], BF16)
                DMA("pool", wba[:], w_in[48][:, :, 0:16], W=["wba"])
                bbba = sbt("bbba", [128, NTILES, 16])
                for t in range(NTILES):
                    bk, bn = (psb[0], "ps0") if t < 32 else (psb[1], "ps1")
                    col = 16 * (t % 32)
                    for kc in range(KC):
                        MM(bk[:, col:col + 16], xbf[:, kc, 128 * t:128 * t + 128], wba[:, kc, :], start=(kc == 0), stop=(kc == KC - 1), R=["wba"] + xbf_all, W=[bn], skip=True)
                CP(bbba[:, 0:32, :], psb[0][:, :].rearrange("p (t c) -> p t c", c=16), W=["ps0", "bbba"])
                CP(bbba[:, 32, :], psb[1][:, 0:16], W=["ps1", "bbba"])
                dtb_s = sbt("dtb_s", [128, 8])
                alog_s = sbt("alog_s", [128, 8])
                DMA("sp", dtb_s[:], dtb_d.partition_broadcast(128), W=["dtb"])
                DMA("sp", alog_s[:], alog_d.partition_broadcast(128), W=["alog"])
                DMA("sp", nw_s[:], nw_d.partition_broadcast(128), W=["nw"])
                DMA("sp", gmask_s[:], gmask_d, W=["gmask"])
                DMA("sp", chm_s[:], chm_d, W=["chm"])
                DMA("sp", cw_s[:].rearrange("p a b -> p (a b)"), cw_d, W=["cw"])
                DMA("sp", stf_s[:].rearrange("p a b -> p (a b)"), stf_d, W=["stf"])
                nexpA = sbt("nexpA", [128, 8])
                ACTF(nexpA[:], alog_s[:], AF.Exp, R=["alog"], W=["nexpA"])
                TS(nexpA[:], nexpA[:], -1.0, None, ALU.mult, W=["nexpA"])
                tma = sbt("tma", [128, NTILES, 8])
                tmb = sbt("tmb", [128, NTILES, 8])
                ACTF(beta[:], bbba[:, :, 0:8], AF.Sigmoid, R=["bbba"], W=["beta"])
                TT(tma[:], bbba[:, :, 8:16], dtb_s[:].unsqueeze(1).to_broadcast([128, NTILES, 8]), ALU.add, R=["bbba", "dtb"], W=["tma"])
                ACTF(tmb[:], tma[:], AF.Abs, R=["tma"], W=["tmb"])
                ACTF(tmb[:], tmb[:], AF.Exp, W=["tmb"], scale=-1.0)
                ACTF(tmb[:], tmb[:], AF.Ln, R=["one"], W=["tmb"], bias=one_t[:])
                TS(tma[:], tma[:], 0.0, None, ALU.max, W=["tma"])
                TT(tma[:], tma[:], tmb[:], ALU.add, R=["tmb"], W=["tma"])
                TT(g[:], tma[:], nexpA[:].unsqueeze(1).to_broadcast([128, NTILES, 8]), ALU.mult, R=["tma", "nexpA"], W=["g"])
                TT(g[:], g[:], gmask_s[:].unsqueeze(2).to_broadcast([128, NTILES, 8]), ALU.mult, R=["gmask"], W=["g"])
                for c in range(2):
                    TS(gsel[:, :, c, :], g[:], chm_s[:, c:c + 1], None, ALU.mult, R=["g", "chm"], W=["gsel"])
                gflat = g[:].rearrange("p t h -> p (t h)")
                MM(psb[2][:, 0:NG], Um, gflat, R=["cm", "g"], W=["ps2"])
                CP(gcol[:], psb[2][:, 0:NG], W=["ps2", "gcol"])
                ACTF(egc[:], gcol[:], AF.Exp, R=["gcol"], W=["egc"])
                TS(negegc[:], egc[:], -1.0, None, ALU.mult, R=["egc"], W=["negegc"])
                MM(psb[3][:, 0:NG], USTm, gflat, R=["cm", "g"], W=["ps3"])
                ACTF(egrem[:], psb[3][:, 0:NG], AF.Exp, W=["ps3", "egrem"])
                gsf = gsel[:].rearrange("p t c h -> p (t c h)")
                MM(psb[2][:, 0:512], ones32, gsf[:, 0:512], R=["cm", "gsel"], W=["ps2"])
                ACTF(egsum[:, 0:512], psb[2][:, 0:512], AF.Exp, W=["ps2", "egsum"])
                MM(psb[3][:, 0:16], ones32, gsf[:, 512:528], R=["cm", "gsel"], W=["ps3"])
                ACTF(egsum[:, 512:528], psb[3][:, 0:16], AF.Exp, W=["ps3", "egsum"])

                esT.close()
                P.fence()
                QnT = sbd("QnT", [128, NT], BF16)
                KnT = sbd("KnT", [128, NT], BF16)
                VT2 = [sbd(f"VT2_{i}", [128, NT], BF16) for i in range(2)]
                wz = wt
                uni = sbd("uni", [128, 1030])
                class Sub:
                    def __init__(self, t, c0, n):
                        self.t, self.c0, self.n = t, c0, n

                    def __getitem__(self, key):
                        rows, cols = key
                        a = (cols.start or 0) + self.c0
                        b = (cols.stop if cols.stop is not None else self.n) + self.c0
                        return self.t[rows, a:b]
                pre = Sub(uni, 0, 515)
                yb = Sub(uni, 515, 512)
                sl = yb
                gbc2 = [sbd(f"gbc{i}", [128, 128]) for i in range(2)]
                Dm = [sbd(f"Dm{i}", [128, 128]) for i in range(2)]
                decT = Dm
                Nm = [sbd(f"Nm{i}", [128, 256], BF16) for i in range(2)]
                uni_b = uni[:, :].bitcast(BF16)
                PP = [[Sub(uni_b, 256 * (2 * i + j), 256) for j in range(2)] for i in range(2)]
                Xb = [[sbd(f"X{i}_{j}", [128, 128], BF16) for j in range(2)] for i in range(2)]
                Xbf = [[sbd(f"Xbf{p}{i}", [128, 128], BF16) for i in range(2)] for p in range(2)]
                attnT = [[sbd(f"attnT{p}{i}", [128, 128], BF16) for i in range(2)] for p in range(2)]
                Kntm = sbd("Kntm", [128, 128], BF16)
                Vtm2 = [[sbd(f"Vtm2_{p}{i}", [128, 128], BF16) for i in range(2)] for p in range(2)]
                kdz = [[[sbd(f"kdz{p}{i}_{c}", [128, 128], BF16) for c in range(2)] for i in range(2)] for p in range(2)]
                Rb = [sbd(f"Rb{i}", [128, 128], BF16) for i in range(2)]
                vn = [sbd(f"vn{i}", [128, 128], BF16) for i in range(2)]
                otmp = [sbd(f"otmp{i}", [128, 128]) for i in range(2)]
                ob = otmp
                S32 = [sbd(f"S32_{i}", [128, 128]) for i in range(2)]
                Sbf = [sbd(f"Sbf{i}", [128, 128], BF16) for i in range(2)]
                ssq2 = [sbd(f"ssq{i}", [128, 2]) for i in range(2)]
                szb2 = [sbd(f"szb{i}", [128, 128], BF16) for i in range(2)]
                onb2 = [sbd(f"onb{i}", [128, 128]) for i in range(2)]
                dnT2 = [sbd(f"dnT{i}", [128, 128], BF16) for i in range(2)]
                for i in range(2):
                    MSET(Rb[i][:], 0.0, W=[f"Rb{i}"])
                    MSET(vn[i][:], 0.0, W=[f"vn{i}"])
                    for c in range(2):
                        for p in range(2):
                            MSET(kdz[p][i][c][:], 0.0, W=[f"kdz{p}{i}"])

                for hq in range(4 if stop_after != "dnpre" else 0):
                    heads = (2 * hq, 2 * hq + 1)
                    specs = [("q", 3072 + 128 * hq, hq), ("k", 3584 + 128 * hq, 4 + hq), ("v0", 4096 + 128 * heads[0], 8 + heads[0]), ("v1", 4096 + 128 * heads[1], 8 + heads[1])]
                    blocks256 = [(256 * i, 256) for i in range(16)] + [(VT, 128)]

                    def proj(which, c0w, cb, slot):
                        pre_ = Sub(uni, slot * 515, 259)
                        yb_ = Sub(uni, slot * 515 + 259, 256)
                        pn, yn, sn_ = f"pre{slot}", f"yb{slot}", f"sqb{slot}"
                        pg, pgn = psb[slot], f"ps{slot}"
                        pq, pqn = psb[3 + slot], f"ps{3 + slot}"
                        sq_ = Sub(Nm[slot], 0, 256)
                        wi = load_w(c0w)
                        yield
                        MSET(pre_[:, 0:3], 0.0, W=[pn])
                        yield
                        for bi, (c0, n) in enumerate(blocks256):
                            for kc in range(KC):
                                MM(pg[:, 0:n], wt[wi][:, kc, :], xbf[:, kc, c0:c0 + n], start=(kc == 0), stop=(kc == KC - 1), R=[f"wt{wi}"] + xbf_all, W=[pgn])
                                if kc % 4 == 3:
                                    yield
                            if bi == 16:
                                CP(pre_[:, 0:3], stf_s[:, cb, :], R=["stf"], W=[pn])
                                yield
                            CP(pre_[:, 3:3 + n], pg[:, 0:n], W=[pgn, pn], eng="act")
                            yield
                            TS(yb_[:, 0:n], pre_[:, 0:n], cw_s[:, cb, 0:1], None, ALU.mult, R=[pn, "cw"], W=[yn])
                            yield
                            for j in range(1, 4):
                                STT(yb_[:, 0:n], pre_[:, j:j + n], cw_s[:, cb, j:j + 1], yb_[:, 0:n], ALU.mult, ALU.add, R=[pn, "cw"], W=[yn])
                                yield
                            if bi == 16:
                                MSET(yb_[:, 1:128], 0.0, W=[yn])
                                yield
                            elif bi < 15:
                                CP(pre_[:, 0:3], pre_[:, n:n + 3], W=[pn])
                                yield
                            if which in ("v0", "v1"):
                                ACTF(VT2[int(which[1])][:, c0:c0 + n], yb_[:, 0:n], AF.Silu, R=[yn], W=["VT2_" + which[1]])
                                yield
                            else:
                                ACTF(yb_[:, 0:n], yb_[:, 0:n], AF.Silu, W=[yn])
                                yield
                                ACTF(sq_[:, 0:n], yb_[:, 0:n], AF.Square, R=[yn], W=[sn_])
                                yield
                                MM(pq[:, 0:n], ones_bf[:], sq_[:, 0:n], R=[sn_, "ones_bf"], W=[pqn])
                                yield
                                ACTF(pre_[:, 3:3 + n], pq[:, 0:n], AF.Sqrt, R=["eps"], W=[pqn, pn], bias=eps_t[:])
                                yield
                                RCP(pre_[:, 3:3 + n], pre_[:, 3:3 + n], W=[pn])
                                yield
                                if which == "q":
                                    STT(QnT[:, c0:c0 + n], yb_[:, 0:n], 128 ** -0.5, pre_[:, 3:3 + n], ALU.mult, ALU.mult, R=[yn, pn], W=["QnT"])
                                else:
                                    TT(KnT[:, c0:c0 + n], yb_[:, 0:n], pre_[:, 3:3 + n], ALU.mult, R=[yn, pn], W=["KnT"])
                                yield

                    for pair in ((specs[0], specs[1]), (specs[2], specs[3])):
                        for _ in itertools.zip_longest(proj(*pair[0], 0), proj(*pair[1], 1)):
                            pass
                    for hh in range(2):
                        DMA("pool", wz[hh][:], w_in[40 + heads[hh]], W=[f"wt{hh}"])
                        MSET(S32[hh][:], 0.0, W=[f"S32_{hh}"])
                        MSET(Sbf[hh][:], 0.0, W=[f"Sbf{hh}"])
                    P.mark(f'dn_proj_done_{hq}')
                    P.fence()
                    p7b = psb[7][:, :].bitcast(BF16)

                    def shared(t):
                        tp = t % 2
                        tsl = slice(128 * t, 128 * t + 128)
                        MM(psb[7][:, 0:128], KnT[:, tsl], KnT[:, tsl], R=["KnT"], W=["ps7"], skip=True)
                        MM(psb[7][:, 128:256], KnT[:, tsl], QnT[:, tsl], R=["KnT", "QnT"], W=["ps7"], skip=True)
                        TRP(p7b[:, 512:640], KnT[:, tsl], ident_bf[:], R=["KnT", "ident_bf"], W=["ps7"])
                        for hh in range(2):
                            TRP(p7b[:, 640 + 128 * hh:768 + 128 * hh], VT2[hh][:, tsl], ident_bf[:], R=[f"VT2_{hh}", "ident_bf"], W=["ps7"])
                        CP(Kntm[:], p7b[:, 512:640], W=["ps7", "Kntm"])
                        for hh in range(2):
                            CP(Vtm2[tp][hh][:], p7b[:, 640 + 128 * hh:768 + 128 * hh], W=["ps7", f"Vtm2_{tp}{hh}"])

                    def prep(t, hh):
                        tp = t % 2
                        h = heads[hh]
                        gbc = gbc2[hh]
                        col = t * 8 + h
                        bA, nA = psb[3 + 2 * hh], f"ps{3 + 2 * hh}"
                        bB, nB = psb[4 + 2 * hh], f"ps{4 + 2 * hh}"
                        aT, aTn = attnT[tp][hh], f"attnT{tp}{hh}"
                        CP(gbc[:], g[:, t, h:h + 1].to_broadcast([128, 128]), R=["g"], W=[f"gbc{hh}"], eng="pool")
                        yield
                        MM(bB[:, 0:128], gbc[:], Um, R=[f"gbc{hh}", "cm"], W=[nB])
                        yield
                        STT(Dm[hh][:], bB[:, 0:128], gcol[:, col:col + 1], MTm, ALU.subtract, ALU.add, R=["gcol", "cm"], W=[nB, f"Dm{hh}"])
                        yield
                        ACTF(Dm[hh][:], Dm[hh][:], AF.Exp, W=[f"Dm{hh}"])
                        yield
                        TT(Nm[hh][:, 128:256], psb[7][:, 0:128], Dm[hh][:], ALU.mult, R=[f"Dm{hh}"], W=["ps7", f"Nm{hh}"])
                        yield
                        STT(Nm[hh][:, 0:128], Nm[hh][:, 128:256], beta[:, t, h:h + 1], SUm, ALU.mult, ALU.mult, R=["beta", "cm"], W=[f"Nm{hh}"])
                        yield
                        TT(aT[:], psb[7][:, 128:256], Dm[hh][:], ALU.mult, R=[f"Dm{hh}"], W=["ps7", aTn])
                        yield
                        bAb = bA[:, :].bitcast(BF16)
                        TRP(bAb[:, 512:640], Nm[hh][:, 0:128], ident_bf[:], R=[f"Nm{hh}", "ident_bf"], W=[nA])
                        yield
                        CP(Nm[hh][:, 128:256], bAb[:, 512:640], W=[nA, f"Nm{hh}"], eng="act")
                        yield
                        TT(Xb[hh][0][:], ident, Nm[hh][:, 0:128], ALU.subtract, R=["cm", f"Nm{hh}"], W=[f"X{hh}_0"], eng="pool")
                        yield
                        cur = Nm[hh]
                        curn = f"Nm{hh}"
                        for k in range(5):
                            nxt = PP[hh][k % 2]
                            nxtn = f"PP{hh}_{k % 2}"
                            if k < 4:
                                MM(bA[:, 0:128], cur[:, 128:256], cur[:, 0:128], R=[curn], W=[nA], skip=True)
                                yield
                                MM(bA[:, 128:256], cur[:, 0:128], cur[:, 128:256], R=[curn], W=[nA], skip=True)
                                yield
                                CP(nxt[:, 0:256], bA[:, 0:256], W=[nA, nxtn], eng="act")
                                yield
                            else:
                                MM(bA[:, 128:256], cur[:, 0:128], cur[:, 128:256], R=[curn], W=[nA], skip=True)
                                yield
                                CP(nxt[:, 128:256], bA[:, 128:256], W=[nA, nxtn], eng="act")
                                yield
                            xo = Xb[hh][k % 2]
                            xn_, xnn = (Xb[hh][(k + 1) % 2], f"X{hh}_{(k + 1) % 2}") if k < 4 else (Xbf[tp][hh], f"Xbf{tp}{hh}")
                            MM(bB[:, 0:128], nxt[:, 128:256], xo[:], R=[nxtn, f"X{hh}_{k % 2}"], W=[nB])
                            yield
                            TT(xn_[:], bB[:, 0:128], xo[:], ALU.add, R=[f"X{hh}_{k % 2}"], W=[nB, xnn])
                            yield
                            cur, curn = nxt, nxtn
                        for c in range(2):
                            rs = slice(64 * c, 64 * c + 64)
                            TS(kdz[tp][hh][c][rs, :], Kntm[rs, :], egrem[rs, col:col + 1], None, ALU.mult, R=["Kntm", "egrem"], W=[f"kdz{tp}{hh}"])
                            yield

                    def chain(t, hh):
                        tp = t % 2
                        tsl = slice(128 * t, 128 * t + 128)
                        h = heads[hh]
                        ssq, szb, onb, dnT = ssq2[hh], szb2[hh], onb2[hh], dnT2[hh]
                        pc, pn_ = psb[hh], f"ps{hh}"
                        col = t * 8 + h
                        if t == 32:
                            DMA("sp", o_pdr[h], S32[hh][:], R=[f"S32_{hh}"], W=["o_pdr"])
                            yield
                            DMA("sp", S32[hh][:], st_rec[h], W=[f"S32_{hh}"])
                            yield
                            CP(Sbf[hh][:], S32[hh][:], R=[f"S32_{hh}"], W=[f"Sbf{hh}"], eng="act")
                            yield
                        for c in range(2):
                            rs = slice(64 * c, 64 * c + 64)
                            MM(pc[:, 0:128], KnT[:, tsl], Sbf[hh][:], R=["KnT", f"Sbf{hh}"], W=[pn_], skip=True)
                            yield
                            MM(pc[:, 128:256], QnT[:, tsl], Sbf[hh][:], R=["QnT", f"Sbf{hh}"], W=[pn_], skip=True)
                            yield
                            STT(Rb[hh][rs, :], pc[rs, 0:128], negegc[rs, col:col + 1], Vtm2[tp][hh][rs, :], ALU.mult, ALU.add, R=["negegc", f"Vtm2_{tp}{hh}"], W=[pn_, f"Rb{hh}"])
                            yield
                            TS(otmp[hh][rs, :], pc[rs, 128:256], egc[rs, col:col + 1], None, ALU.mult, R=["egc"], W=[pn_, f"otmp{hh}"])
                            yield
                            MM(pc[:, 256:384], Xbf[tp][hh][:], Rb[hh][:], R=[f"Xbf{tp}{hh}", f"Rb{hh}"], W=[pn_], skip=True)
                            yield
                            TS(vn[hh][rs, :], pc[rs, 256:384], beta[rs, t, h:h + 1], None, ALU.mult, R=["beta"], W=[pn_, f"vn{hh}"])
                            yield
                            MM(pc[:, 384:512], kdz[tp][hh][c][:], vn[hh][:], R=[f"kdz{tp}{hh}", f"vn{hh}"], W=[pn_], skip=True)
                            yield
                            ec = t * 16 + c * 8 + h
                            STT(S32[hh][:], S32[hh][:], egsum[:, ec:ec + 1], pc[:, 384:512], ALU.mult, ALU.add, R=["egsum"], W=[pn_, f"S32_{hh}"])
                            yield
                            CP(Sbf[hh][:], S32[hh][:], R=[f"S32_{hh}"], W=[f"Sbf{hh}"], eng="act")
                            yield
                        if t >= 23:
                            MM(pc[:, 0:128], attnT[tp][hh][:], vn[hh][:], R=[f"attnT{tp}{hh}", f"vn{hh}"], W=[pn_], skip=True)
                            yield
                            TT(otmp[hh][:], pc[:, 0:128], otmp[hh][:], ALU.add, W=[pn_, f"otmp{hh}"])
                            yield
                            for kc in range(KC):
                                MM(pc[:, 128:256], xbf[:, kc, tsl], wz[hh][:, kc, :], start=(kc == 0), stop=(kc == KC - 1), R=[f"wt{hh}"] + xbf_all, W=[pn_], skip=True)
                                if kc % 4 == 3:
                                    yield
                            ACTF(szb[:], pc[:, 128:256], AF.Silu, W=[pn_, f"szb{hh}"])
                            yield
                            MSET(ssq[:, 0:1], 0.0, W=[f"ssq{hh}"], eng="pool")
                            yield
                            ACTF(onb[:], otmp[hh][:], AF.Square, R=[f"otmp{hh}"], W=[f"onb{hh}", f"ssq{hh}"], accum=ssq[:, 0:1])
                            yield
                            ACTF(ssq[:, 1:2], ssq[:, 0:1], AF.Sqrt, R=["eps"], W=[f"ssq{hh}"], bias=eps_t[:], scale=1.0 / 128)
                            yield
                            RCP(ssq[:, 1:2], ssq[:, 1:2], W=[f"ssq{hh}"])
                            yield
                            STT(onb[:], otmp[hh][:], ssq[:, 1:2], nw_s[:], ALU.mult, ALU.mult, R=[f"otmp{hh}", f"ssq{hh}", "nw"], W=[f"onb{hh}"])
                            yield
                            TT(onb[:], onb[:], szb[:], ALU.mult, R=[f"szb{hh}"], W=[f"onb{hh}"], eng="pool")
                            yield
                            TRP(pc[:, 256:384], onb[:], ident, R=[f"onb{hh}", "cm"], W=[pn_])
                            yield
                            CP(dnT[:], pc[:, 256:384], W=[pn_, f"dnT{hh}"])
                            yield
                            DMA("sp", mixD[1024 + 128 * h:1024 + 128 * h + 128, 128 * t - P2_0:128 * t - P2_0 + 128], dnT[:], R=[f"dnT{hh}"], W=["mixD"])
                            yield

                    shared(0)
                    for step in range(-1, NTILES):
                        if copy_chunks:
                            da_, sa_ = copy_chunks.pop()
                            DMA("sp", da_, sa_, W=["o_sw"])
                        if step + 1 < NTILES and step + 1 > 0:
                            shared(step + 1)
                        cgens = [chain(step, 0), chain(step, 1)] if step >= 0 else []
                        pgens = [prep(step + 1, 0), prep(step + 1, 1)] if step + 1 < NTILES else []
                        live = [[g_, 2] for g_ in cgens] + [[g_, 1] for g_ in pgens]
                        while live:
                            for ent in list(live):
                                for _ in range(ent[1]):
                                    try:
                                        next(ent[0])
                                    except StopIteration:
                                        live.remove(ent)
                                        break
                    P.fence()
                    for hh in range(2):
                        DMA("sp", o_sdr[heads[hh]], S32[hh][:], R=[f"S32_{hh}"], W=["o_sdr"])
            P.fence()
        while copy_chunks:
            da_, sa_ = copy_chunks.pop()
            DMA("sp", da_, sa_, W=["o_sw"])
        esX.close()
        P.fence()
        if stop_after not in ("att", "dn", "dnpre"):
            PC0, PC1 = 126, P2N - 127
            TB2 = [(126, 512), (638, 512), (1150, 3)]
            h2bf = sb("h2bf", [128, KC, P2N], BF16)
            r2 = sb("r2", [128, P2N])
            lnp = sb("lnp_s", [128, 3 * KC])
            DMA("sp", lnp[:], lnp_d, W=["lnp"])

            def rstd_from(acc_banks, dst, scale_div):
                for bi_, (c0, n) in enumerate(TB2):
                    bk = psb[5 + bi_]
                    ACTF(dst[:, c0:c0 + n], bk[:, 0:n], AF.Sqrt, R=["eps"], W=[f"ps{5 + bi_}", "r2"], bias=eps_t[:], scale=1.0 / scale_div)
                RCP(dst[:, PC0:PC1], dst[:, PC0:PC1], W=["r2"])

            with ExitStack() as esP:
                sbp = lambda name, shape, dt=F32: esP.enter_context(nc.sbuf_tensor(name, list(shape), dt))
                mixb = sbp("mixb", [128, KC, P2N], BF16)
                X1 = sbp("X1", [128, KC, P2N])
                wo = [sbp(f"wo{i}", [128, KC, 128], BF16) for i in range(2)]
                sq2 = [sbp(f"sq2_{i}", [128, 512], BF16) for i in range(2)]
                xs = [sbp(f"xs{i}", [128, P2N]) for i in range(2)]
                for kc in range(KC):
                    DMA("sp" if kc % 2 == 0 else "act", mixb[:, kc, :], mixD[128 * kc:128 * kc + 128, :], R=["mixD"], W=[f"mixb{kc}"])
                mixb_all = [f"mixb{kc}" for kc in range(KC)]
                nsq = 0
                for ob in range(KC):
                    wi = ob % 2
                    DMA("pool", wo[wi][:], w_out[ob], W=[f"wo{wi}"])
                    for bi_, (c0, n) in enumerate(TB2):
                        for kc in range(KC):
                            MM(psb[bi_][:, 0:n], wo[wi][:, kc, :], mixb[:, kc, c0:c0 + n], start=(kc == 0), stop=(kc == KC - 1), R=[f"wo{wi}"] + mixb_all, W=[f"ps{bi_}"])
                        CP(X1[:, ob, c0:c0 + n], psb[bi_][:, 0:n], W=[f"ps{bi_}", f"X1_{ob}"], eng="act")
                        si = nsq % 2
                        nsq += 1
                        ACTF(sq2[si][:, 0:n], X1[:, ob, c0:c0 + n], AF.Square, R=[f"X1_{ob}"], W=[f"sq2_{si}"])
                        MM(psb[5 + bi_][:, 0:n], ones_bf[:], sq2[si][:, 0:n], start=(ob == 0), stop=(ob == KC - 1), R=[f"sq2_{si}", "ones_bf"], W=[f"ps{5 + bi_}"])
                rstd_from(None, r2, D)
                def p2_load(ob):
                    DMA("sp", xs[ob % 2][:, PC0:PC1], xT[128 * ob:128 * ob + 128, P2_0 + PC0:P2_0 + PC1], W=[f"xs{ob % 2}"])

                def p2_comp(ob):
                    nonlocal nsq
                    xi = ob % 2
                    STT(X1[:, ob, PC0:PC1], X1[:, ob, PC0:PC1], lnp[:, ob:ob + 1], r2[:, PC0:PC1], ALU.mult, ALU.mult, R=["lnp", "r2"], W=[f"X1_{ob}"])
                    TT(X1[:, ob, PC0:PC1], X1[:, ob, PC0:PC1], xs[xi][:, PC0:PC1], ALU.add, R=[f"xs{xi}"], W=[f"X1_{ob}"])
                    DMA("act", x1D[128 * ob:128 * ob + 128, 128:PC1], X1[:, ob, 128:PC1], R=[f"X1_{ob}"], W=["x1D"])
                    for bi_, (c0, n) in enumerate(TB2):
                        si = nsq % 2
                        nsq += 1
                        ACTF(sq2[si][:, 0:n], X1[:, ob, c0:c0 + n], AF.Square, R=[f"X1_{ob}"], W=[f"sq2_{si}"])
                        MM(psb[5 + bi_][:, 0:n], ones_bf[:], sq2[si][:, 0:n], start=(ob == 0), stop=(ob == KC - 1), R=[f"sq2_{si}", "ones_bf"], W=[f"ps{5 + bi_}"])

                p2_load(0)
                for ob in range(KC):
                    if ob + 1 < KC:
                        p2_load(ob + 1)
                    p2_comp(ob)
                rstd_from(None, r2, D)
                for ob in range(KC):
                    STT(h2bf[:, ob, PC0:PC1], X1[:, ob, PC0:PC1], lnp[:, KC + ob:KC + ob + 1], r2[:, PC0:PC1], ALU.mult, ALU.mult, R=[f"X1_{ob}", "lnp", "r2"], W=[f"h2bf{ob}"])
            P.fence()
            h2_all = [f"h2bf{ob}" for ob in range(KC)]

            P.mark('p2_pass3')
            NJ = DFF // 128
            with ExitStack() as esF:
                sbf_ = lambda name, shape, dt=F32: esF.enter_context(nc.sbuf_tensor(name, list(shape), dt))
                YC0, YC1 = 128, P2N - 127
                YN = YC1 - YC0
                hid = sbf_("hid", [128, NJ, YN], BF16)
                esG = ExitStack()
                sbg = lambda name, shape, dt=F32: esG.enter_context(nc.sbuf_tensor(name, list(shape), dt))
                wg = [sbg(f"wg{i}", [128, KC, 128], BF16) for i in range(2)]
                wv = [sbg(f"wv{i}", [128, KC, 128], BF16) for i in range(2)]
                upg = sbg("upg", [128, P2N + 2])
                upv = sbg("upv", [128, P2N + 2])
                cg = sbg("cg", [128, P2N])
                cv = sbg("cv", [128, P2N])
                ug = sbg("ug", [128, P2N])
                fcw = sbg("fcw_s", [128, 2 * NJ, 3])
                fcb = sbg("fcb_s", [128, 2 * NJ])
                fst = sbg("fst_s", [128, 2 * NJ, 2])
                ysm = sbg("ysm", [128, 2])
                DMA("sp", fcw[:].rearrange("p a b -> p (a b)"), fcw_d, W=["fcw"])
                DMA("sp", fcb[:], fcb_d, W=["fcb"])
                DMA("sp", fst[:].rearrange("p a b -> p (a b)"), fst_d, W=["fst"])
                MSET(upg[:, 0:2], 0.0, W=["upg"])
                MSET(upv[:, 0:2], 0.0, W=["upv"])
                GC = 2.0 * math.sqrt(2.0 / math.pi)
                SC = P2N - 128
                for j in range(NJ):
                    wi = j % 2
                    DMA("pool", wg[wi][:], w_ffn_in[j], W=[f"wg{wi}"])
                    DMA("pool", wv[wi][:], w_ffn_in[DFF // 128 + j], W=[f"wv{wi}"])
                    for gv, (wtile, wn, up, upn) in enumerate(((wg[wi], f"wg{wi}", upg, "upg"), (wv[wi], f"wv{wi}", upv, "upv"))):
                        for bi_, (c0, n) in enumerate(TB2):
                            pbi = 3 * gv + bi_
                            for kc in range(KC):
                                MM(psb[pbi][:, 0:n], wtile[:, kc, :], h2bf[:, kc, c0:c0 + n], start=(kc == 0), stop=(kc == KC - 1), R=[wn] + h2_all, W=[f"ps{pbi}"])
                            CP(up[:, 2 + c0:2 + c0 + n], psb[pbi][:, 0:n], W=[f"ps{pbi}", upn], eng="act")
                        jj = gv * NJ + j
                        DMA("sp", o_pfc[gv, j], up[:, 2 + SC - 2:2 + SC], R=[upn], W=["o_pfc"])
                        DMA("sp", o_sfc[1:2, gv * DFF + 128 * j:gv * DFF + 128 * j + 128].rearrange("a d -> d a"), up[:, 2 + SC:2 + SC + 1], R=[upn], W=["o_sfc"])
                        cdst = cg if gv == 0 else cv
                        cn = "cg" if gv == 0 else "cv"
                        TS(cdst[:, YC0:YC1], up[:, YC0:YC1], fcw[:, jj, 0:1], fcb[:, jj:jj + 1], ALU.mult, ALU.add, R=[upn, "fcw", "fcb"], W=[cn])
                        STT(cdst[:, YC0:YC1], up[:, YC0 + 1:YC1 + 1], fcw[:, jj, 1:2], cdst[:, YC0:YC1], ALU.mult, ALU.add, R=[upn, "fcw"], W=[cn])
                        STT(cdst[:, YC0:YC1], up[:, YC0 + 2:YC1 + 2], fcw[:, jj, 2:3], cdst[:, YC0:YC1], ALU.mult, ALU.add, R=[upn, "fcw"], W=[cn])
                        TS(ysm[:, 0:1], fst[:, jj, 0:1], fcw[:, jj, 0:1], fcb[:, jj:jj + 1], ALU.mult, ALU.add, R=["fst", "fcw", "fcb"], W=["ysm"])
                        STT(ysm[:, 0:1], fst[:, jj, 1:2], fcw[:, jj, 1:2], ysm[:, 0:1], ALU.mult, ALU.add, R=["fst", "fcw"], W=["ysm"])
                        STT(cdst[:, SC:SC + 1], up[:, 2 + SC:2 + SC + 1], fcw[:, jj, 2:3], ysm[:, 0:1], ALU.mult, ALU.add, R=[upn, "fcw", "ysm"], W=[cn])
                    ACTF(ug[:, YC0:YC1], cg[:, YC0:YC1], AF.Square, R=["cg"], W=["ug"])
                    TS(ug[:, YC0:YC1], ug[:, YC0:YC1], 0.044715, 1.0, ALU.mult, ALU.add, W=["ug"])
                    TT(ug[:, YC0:YC1], ug[:, YC0:YC1], cg[:, YC0:YC1], ALU.mult, R=["cg"], W=["ug"])
                    ACTF(ug[:, YC0:YC1], ug[:, YC0:YC1], AF.Sigmoid, W=["ug"], scale=GC)
                    TT(ug[:, YC0:YC1], ug[:, YC0:YC1], cg[:, YC0:YC1], ALU.mult, R=["cg"], W=["ug"])
                    TT(hid[:, j, :], ug[:, YC0:YC1], cv[:, YC0:YC1], ALU.mult, R=["ug", "cv"], W=[f"hid{j}"])
                hid_all = [f"hid{j}" for j in range(NJ)]
                P.mark('ffn_in_done')
                esG.close()
                P.fence()
                TBY = [(0, 512), (512, 512), (1024, 1)]
                fA = h2bf[:, :, :].rearrange("p a b -> p (a b)").bitcast(F32)
                NA = (KC * P2N // 2) // YN
                fB = sbf_("fB", [128, KC - NA, YN])
                fblk = lambda ob: (fA[:, YN * ob:YN * ob + YN] if ob < NA else fB[:, ob - NA, :])
                cvf = [sbf_(f"cvf{i}", [128, YN]) for i in range(2)]
                wf = [sbf_(f"wf{i}", [128, NJ, 128], BF16) for i in range(2)]
                sq4 = [sbf_(f"sq4_{i}", [128, 512], BF16) for i in range(2)]
                nf = 0
                for ob in range(KC):
                    wi = ob % 2
                    DMA("pool", wf[wi][:], w_ffn_out[ob], W=[f"wf{wi}"])
                    fb = fblk(ob)
                    for bi_, (c0, n) in enumerate(TBY):
                        for kc in range(NJ):
                            MM(psb[bi_][:, 0:n], wf[wi][:, kc, :], hid[:, kc, c0:c0 + n], start=(kc == 0), stop=(kc == NJ - 1), R=[f"wf{wi}"] + hid_all, W=[f"ps{bi_}"])
                        fi = nf % 2
                        nf += 1
                        CP(fb[:, c0:c0 + n], psb[bi_][:, 0:n], W=[f"ps{bi_}", f"f{ob}"], eng="act")
                        ACTF(sq4[fi][:, 0:n], fb[:, c0:c0 + n], AF.Square, R=[f"f{ob}"], W=[f"sq4_{fi}"])
                        MM(psb[5 + bi_][:, 0:n], ones_bf[:], sq4[fi][:, 0:n], start=(ob == 0), stop=(ob == KC - 1), R=[f"sq4_{fi}", "ones_bf"], W=[f"ps{5 + bi_}"])
                for bi_, (c0, n) in enumerate(TBY):
                    ACTF(r2[:, c0:c0 + n], psb[5 + bi_][:, 0:n], AF.Sqrt, R=["eps"], W=[f"ps{5 + bi_}", "r2"], bias=eps_t[:], scale=1.0 / D)
                RCP(r2[:, 0:YN], r2[:, 0:YN], W=["r2"])

                def fin_load(ob):
                    DMA("act" if ob % 2 == 0 else "sp", cvf[ob % 2][:], x1D[128 * ob:128 * ob + 128, YC0:YC1], R=["x1D"], W=[f"cvf{ob % 2}"])

                def fin_comp(ob):
                    fb = fblk(ob)
                    STT(fb, fb, lnp[:, 2 * KC + ob:2 * KC + ob + 1], r2[:, 0:YN], ALU.mult, ALU.mult, R=["lnp", "r2"], W=[f"f{ob}"])
                    TT(fb, fb, cvf[ob % 2][:], ALU.add, R=[f"cvf{ob % 2}"], W=[f"f{ob}"])
                    DMA("sp" if ob % 2 == 0 else "act", o_y[128 * ob:128 * ob + 128, :], fb, R=[f"f{ob}"], W=["o_y"])

                fin_load(0)
                for ob in range(KC):
                    if ob + 1 < KC:
                        fin_load(ob + 1)
                    fin_comp(ob)
        if debug:
            dbg = sb("dbg", [128, 256], BF16)
            dbg32 = sb("dbg32", [128, 256], F32)
            for rb_ in range(16):
                for ck in range(5):
                    DMA("sp", dbg[:], mixD[128 * rb_:128 * rb_ + 128, 256 * ck:256 * ck + 256], R=["mixD"], W=["dbg"])
                    CP(dbg32[:], dbg[:], R=["dbg"], W=["dbg32"])
                    DMA("sp", o_mix[128 * rb_:128 * rb_ + 128, 256 * ck:256 * ck + 256], dbg32[:], R=["dbg32"], W=["o_mix"])
        P.mark('end')
        P.finish()
    nc._marks = P.marks
    return nc


def tile_weight(w):
    K_, C_ = w.shape
    Cp = -(-C_ // 128) * 128
    if Cp != C_:
        w = np.concatenate([w, np.zeros((K_, Cp - C_), w.dtype)], axis=1)
    return np.ascontiguousarray(w.reshape(K_ // 128, 128, Cp // 128, 128).transpose(2, 1, 0, 3))


def prep_weights(inputs):
    return {n: tile_weight(np.asarray(inputs[n][0])) for n in ("w_in", "w_out", "w_ffn_in", "w_ffn_out")}


def prep_core_inputs(c, inputs, consts, consts_w=None):
    if consts_w is None:
        consts_w = prep_weights(inputs)
    cm, ch, oh, negv = consts
    b, j = c // 4, c % 4
    pad = (3 - j) * 1024
    xv = np.zeros((NT, D), np.float32)
    xv[pad:VT] = inputs["x_prompt"][b, 0:VT - pad]
    xv[VT] = inputs["x_sample"][c, 0]
    kmb = np.zeros((128, 96), np.float32)
    p = np.arange(128)
    for bi, dil in enumerate(DILS):
        for r in range(dil):
            for n in range(32 // dil):
                tok = dil * (128 * n + p) + r
                col = {0: n, 1: 32 + r * 8 + n, 2: 64 + r * 2 + n}[bi]
                kmb[:, col] = np.where(tok < pad, NEG, 0.0)
    gmask = np.ones((128, NTILES), np.float32)
    gmask[1:, NTILES - 1] = 0.0
    fm = lambda a: np.ascontiguousarray(a.reshape(a.shape[0], -1, 128).transpose(2, 1, 0).reshape(128, -1))
    return {
        "xT": np.ascontiguousarray(xv.T),
        "kmb": kmb,
        "cm": cm, "oh": oh, "negv": negv, "chm": ch, "gmask": gmask,
        "rel_bias": np.ascontiguousarray(inputs["rel_bias"]),
        "ln1": np.ascontiguousarray(inputs["ln_mix_pre"][0].reshape(KC, 128).T),
        "w_in": consts_w["w_in"],
        "cache_k": np.ascontiguousarray(inputs["cache_win_k"][0, c].reshape(2048, 1024)),
        "cache_v": np.ascontiguousarray(inputs["cache_win_v"][0, c].reshape(2048, 1024)),
        "st_dnc": np.ascontiguousarray(inputs["state_dn_conv"][0, c]),
        "cw": fm(inputs["dn_conv_w"][0]),
        "stf": fm(inputs["state_dn_conv"][0, c]),
        "alog": np.ascontiguousarray(inputs["dn_A_log"][0].reshape(1, 8)),
        "dtb": np.ascontiguousarray(inputs["dn_dt_bias"][0].reshape(1, 8)),
        "nw": np.ascontiguousarray(inputs["dn_norm_w"][0].reshape(1, 128)),
        "st_rec": np.ascontiguousarray(inputs["state_dn_rec"][0, c]),
        "w_out": consts_w["w_out"],
        "w_ffn_in": consts_w["w_ffn_in"],
        "w_ffn_out": consts_w["w_ffn_out"],
        "lnp": np.ascontiguousarray(np.concatenate([inputs[n][0].reshape(KC, 128).T for n in ("ln_mix_post", "ln_ffn_pre", "ln_ffn_post")], axis=1)),
        "fcw": fm(inputs["ffn_conv_w"][0]),
        "fcb": fm(inputs["ffn_conv_b"][0].reshape(1, -1)),
        "fst": fm(inputs["state_ffn_conv"][0, c]),
        "st_ffc": np.ascontiguousarray(inputs["state_ffn_conv"][0, c]),
    }


def kernel(**inputs):
    inputs = {k: np.asarray(v) for k, v in inputs.items()}
    consts = host_consts()
    nc = build(debug=False)
    consts_w = prep_weights(inputs)
    in_maps = [prep_core_inputs(c, inputs, consts, consts_w) for c in range(NCORES)]
    res = run_bass_kernel_spmd(nc, in_maps, core_ids=list(range(NCORES)))
    R = res.results
    f32 = np.float32
    y_prompt = np.zeros((2, 4096, D), f32)
    y_sample = np.zeros((8, 1, D), f32)
    p_wk = np.zeros((1, 2, 2048, 8, 128), f32)
    p_wv = np.zeros((1, 2, 2048, 8, 128), f32)
    p_dc = np.zeros((1, 2, 3, 2048), f32)
    p_dr = np.zeros((1, 2, 8, 128, 128), f32)
    p_fc = np.zeros((1, 2, 2, 2 * DFF), f32)
    s_wk = np.zeros((1, 8, 2048, 8, 128), f32)
    s_wv = np.zeros((1, 8, 2048, 8, 128), f32)
    s_dc = np.zeros((1, 8, 3, 2048), f32)
    s_dr = np.zeros((1, 8, 8, 128, 128), f32)
    s_fc = np.zeros((1, 8, 2, 2 * DFF), f32)
    for b in range(2):
        r = R[4 * b + 3]
        p_wk[0, b] = np.transpose(r["o_wk"], (2, 0, 1))
        p_wv[0, b] = np.transpose(r["o_wv"], (2, 0, 1))
        p_dc[0, b] = np.transpose(r["o_dnc"][:, :, 0:3], (2, 0, 1)).reshape(3, 2048)
        p_dr[0, b] = r["o_pdr"]
        p_fc[0, b] = np.transpose(r["o_pfc"], (3, 0, 1, 2)).reshape(2, 2 * DFF)
    for c in range(NCORES):
        s_wk[0, c] = R[c]["o_swk"].reshape(2048, 8, 128)
        s_wv[0, c] = R[c]["o_swv"].reshape(2048, 8, 128)
        s_dc[0, c] = R[c]["o_sdc"]
        s_dr[0, c] = R[c]["o_sdr"]
        s_fc[0, c] = R[c]["o_sfc"]
        yT = R[c]["o_y"]
        y_prompt[c // 4, 1024 * (c % 4):1024 * (c % 4) + 1024] = yT[:, 0:1024].T
        y_sample[c, 0] = yT[:, 1024]
    return (y_prompt, y_sample, p_wk, p_wv, p_dc, p_dr, p_fc, s_wk, s_wv, s_dc, s_dr, s_fc)
```

```python
import math
import itertools
from contextlib import ExitStack

import numpy as np
import concourse.bass as bass
import concourse.mybir as mybir
from concourse.bass_utils import run_bass_kernel_spmd

F32 = mybir.dt.float32
BF16 = mybir.dt.bfloat16
ALU = mybir.AluOpType
AF = mybir.ActivationFunctionType
AX = mybir.AxisListType

NCORES = 8
D = 2048
KC = 16
VT = 4096
NT = VT + 128
NTILES = NT // 128
P2_0 = 2944
P2N = NT - P2_0
PROJ = 6160
DFF = 5632
EPS = 1e-6
NEG = -1e30
DILS = (1, 4, 16)
ENGS = ("pe", "act", "dve", "pool", "sp")


class Prog:
    NDMA = 10

    def __init__(self, nc, es):
        self.nc = nc
        self.streams = {e: [] for e in ENGS}
        self.count = {e: 0 for e in ENGS}
        self.sem = {e: es.enter_context(nc.semaphore("s_" + e)) for e in ("pe", "act", "dve", "pool")}
        self.dsem, self.dcnt, self.dlast, self.dnum = {}, {}, {}, {}
        for q in ("sp", "pool", "act"):
            self.dsem[q] = [es.enter_context(nc.semaphore(f"d_{q}{i}")) for i in range(self.NDMA)]
            self.dcnt[q] = [0] * self.NDMA
            self.dlast[q] = [None] * self.NDMA
            self.dnum[q] = 0
        self.last_w, self.readers = {}, {}
        self.seen = {e: {} for e in ENGS}
        self.fence_toks = []
        self.marks = []

    def _tw(self, t):
        if t[0] == "c":
            return self.sem[t[1]], t[2], "c_" + t[1]
        return self.dsem[t[1]][t[2]], t[3], f"d_{t[1]}{t[2]}"

    def _deps(self, eng, reads, writes, extra=()):
        toks = list(extra) + self.fence_toks
        for r in reads:
            t = self.last_w.get(r)
            if t is not None:
                toks.append(t)
        for w in writes:
            t = self.last_w.get(w)
            if t is not None:
                toks.append(t)
            toks.extend(self.readers.get(w, ()))
        waits = {}
        for t in toks:
            if t[0] == "c" and t[1] == "pe" and eng == "pe":
                continue
            s, v, name = self._tw(t)
            if self.seen[eng].get(name, 0) >= v:
                continue
            if name not in waits or waits[name][1] < v:
                waits[name] = (s, v)
        for name, (s, v) in waits.items():
            self.seen[eng][name] = v
        return list(waits.values())

    def _commit(self, tok, reads, writes):
        for r in reads:
            self.readers.setdefault(r, []).append(tok)
        for w in writes:
            self.last_w[w] = tok
            self.readers[w] = []

    def op(self, eng, fn, reads=(), writes=()):
        waits = self._deps(eng, reads, writes)
        self.count[eng] += 1
        tok = ("c", eng, self.count[eng])
        self.streams[eng].append((waits, fn, (self.sem[eng], 1)))
        self._commit(tok, reads, writes)

    def dma(self, q, fn, reads=(), writes=()):
        i = self.dnum[q] % self.NDMA
        self.dnum[q] += 1
        extra = [self.dlast[q][i]] if self.dlast[q][i] is not None else []
        waits = self._deps(q, reads, writes, extra)
        self.dcnt[q][i] += 16
        tok = ("d", q, i, self.dcnt[q][i])
        self.dlast[q][i] = tok
        self.streams[q].append((waits, fn, (self.dsem[q][i], 16)))
        self._commit(tok, reads, writes)

    def mark(self, label):
        self.marks.append((label, dict(self.count)))

    def fence(self):
        toks = []
        for e in ("pe", "act", "dve", "pool"):
            if self.count[e]:
                toks.append(("c", e, self.count[e]))
        for q in self.dsem:
            for i in range(self.NDMA):
                if self.dcnt[q][i]:
                    toks.append(("d", q, i, self.dcnt[q][i]))
        self.fence_toks = toks

    def finish(self):
        nc = self.nc
        final = []
        for q in self.dsem:
            for i in range(self.NDMA):
                if self.dcnt[q][i]:
                    final.append((self.dsem[q][i], self.dcnt[q][i]))
        for e in ("pe", "act", "dve", "pool"):
            if self.count[e]:
                final.append((self.sem[e], self.count[e]))
        streams = self.streams

        def emit(eng, lst, tail=()):
            for waits, fn, inc in lst:
                for s, v in waits:
                    eng.wait_ge(s, v)
                fn(eng).then_inc(inc[0], inc[1])
            for s, v in tail:
                eng.wait_ge(s, v)

        with nc.Block() as block:
            @block.sync
            def _(e):
                emit(e, streams["sp"], final)

            @block.tensor
            def _(e):
                emit(e, streams["pe"])

            @block.scalar
            def _(e):
                emit(e, streams["act"])

            @block.vector
            def _(e):
                emit(e, streams["dve"])

            @block.gpsimd
            def _(e):
                emit(e, streams["pool"])


def rel_bucket_np(dist):
    max_exact = 16
    d = np.maximum(dist, 1).astype(np.float32)
    large = max_exact + (np.log(d / max_exact) / np.float32(math.log(2048 / max_exact)) * (32 - max_exact)).astype(np.int32)
    large = np.minimum(large, 31)
    return np.where(dist < max_exact, dist, large)


def host_consts():
    k = np.arange(128)[:, None]
    i = np.arange(128)[None, :]
    same = (k // 64) == (i // 64)
    c = {}
    c["ident"] = np.eye(128, dtype=np.float32)
    c["ones"] = np.ones((128, 128), np.float32)
    c["U"] = (same & (k <= i)).astype(np.float32)
    c["UST"] = (same & (k > i)).astype(np.float32)
    c["MT"] = np.where(same & (i >= k), 0.0, NEG).astype(np.float32)
    c["SU"] = (same & (i > k)).astype(np.float32)
    cm = np.concatenate([c[n] for n in ("ident", "ones", "U", "UST", "MT", "SU")], axis=1)
    ch = np.stack([(np.arange(128) < 64), (np.arange(128) >= 64)], 1).astype(np.float32)
    oh = np.zeros((32, 3, 384), np.float32)
    negv = np.full((3, 384), NEG, np.float32)
    for b, dil in enumerate(DILS):
        dl = np.arange(129)
        bk = rel_bucket_np(dl * dil)
        oh[bk, b, 128 + dl] = 1.0
        negv[b, 128:257] = 0.0
    return cm, ch, oh.reshape(32, 3 * 384), negv.reshape(1, 3 * 384)


def att_units():
    ranges = []
    for m in range(3):
        base = 2560 + 512 * m
        units = []
        for n in range(20 + 4 * m, 24 + 4 * m):
            units.append(dict(b=0, dil=1, r=0, n=n, g=None, q0=128 * n, nq=128, ocol=128 * n - base, ostep=1, msl=(0, 128)))
        for r in range(4):
            n = 5 + m
            units.append(dict(b=1, dil=4, r=r, n=n, g=None, q0=4 * 128 * n + r, nq=128, ocol=r, ostep=4, msl=(0, 128)))
        for r in range(16):
            g = 1 + m
            units.append(dict(b=2, dil=16, r=r, n=1, g=g, q0=16 * (128 + 32 * g) + r, nq=32, ocol=r, ostep=16, msl=(32 * g, 32 * g + 32)))
        ranges.append((base, units))
    return ranges


def key_blocks():
    kb = {}
    for n in range(19, 32):
        kb[(0, 0, n)] = len(kb)
    for r in range(4):
        for n in range(4, 8):
            kb[(1, r, n)] = len(kb)
    for r in range(16):
        for n in range(2):
            kb[(2, r, n)] = len(kb)
    return kb


def build(debug=False, stop_after=None):
    nc = bass.Bass("TRN2", target_bir_lowering=False)
    din = lambda name, shape, dt=F32: nc.dram_tensor(name, list(shape), dt, kind="ExternalInput").ap()
    dout = lambda name, shape, dt=F32: nc.dram_tensor(name, list(shape), dt, kind="ExternalOutput").ap()
    dscr = lambda name, shape, dt=F32: nc.dram_tensor(name, list(shape), dt, kind="Internal").ap()

    xT = din("xT", [D, NT])
    kmb = din("kmb", [128, 96])
    cm_d = din("cm", [128, 768])
    oh_d = din("oh", [32, 3 * 384])
    negv_d = din("negv", [1, 3 * 384])
    rel_bias = din("rel_bias", [32, 8])
    ln1 = din("ln1", [128, KC])
    w_in = din("w_in", [49, 128, KC, 128])
    cache_k = din("cache_k", [2048, 1024])
    cache_v = din("cache_v", [2048, 1024])
    st_dnc = din("st_dnc", [3, 2048])
    cw_d = din("cw", [128, 16 * 4])
    stf_d = din("stf", [128, 16 * 3])
    alog_d = din("alog", [1, 8])
    dtb_d = din("dtb", [1, 8])
    nw_d = din("nw", [1, 128])
    gmask_d = din("gmask", [128, NTILES])
    chm_d = din("chm", [128, 2])
    st_rec = din("st_rec", [8, 128, 128])

    o_wk = dout("o_wk", [8, 128, 2048])
    o_wv = dout("o_wv", [8, 128, 2048])
    o_swk = dout("o_swk", [2048, 1024])
    o_swv = dout("o_swv", [2048, 1024])
    o_dnc = dout("o_dnc", [16, 128, 4])
    o_sdc = dout("o_sdc", [3, 2048])
    o_pdr = dout("o_pdr", [8, 128, 128])
    o_sdr = dout("o_sdr", [8, 128, 128])
    w_out = din("w_out", [KC, 128, KC, 128])
    w_ffn_in = din("w_ffn_in", [2 * DFF // 128, 128, KC, 128])
    w_ffn_out = din("w_ffn_out", [KC, 128, DFF // 128, 128])
    lnp_d = din("lnp", [128, 3 * KC])
    fcw_d = din("fcw", [128, 88 * 3])
    fcb_d = din("fcb", [128, 88])
    fst_d = din("fst", [128, 88 * 2])
    st_ffc = din("st_ffc", [2, 2 * DFF])
    o_y = dout("o_y", [D, P2N - 255])
    o_pfc = dout("o_pfc", [2, DFF // 128, 128, 2])
    o_sfc = dout("o_sfc", [2, 2 * DFF])
    x1D = dscr("x1D", [D, P2N])
    o_mix = dout("o_mix", [D, P2N]) if debug else None

    mixD = dscr("mixD", [D, P2N], BF16)
    Yscr = dscr("Yscr", [128, 384])
    rowD = dscr("rowD", [3, 1024])

    KB = key_blocks()
    RANGES = att_units()

    with ExitStack() as es:
        P = Prog(nc, es)
        sb = lambda name, shape, dt=F32: es.enter_context(nc.sbuf_tensor(name, list(shape), dt))
        psb = [es.enter_context(nc.psum_tensor(f"ps{i}", [128, 512], F32)) for i in range(8)]

        def MM(out, lhsT, rhs, start=True, stop=True, R=(), W=(), skip=False):
            P.op("pe", lambda e: e.matmul(out, lhsT=lhsT, rhs=rhs, start=start, stop=stop, skip_group_check=skip), reads=R, writes=W)

        def TRP(out, in_, idn, R=(), W=()):
            P.op("pe", lambda e: e.transpose(out, in_, idn), reads=R, writes=W)

        def ACTF(out, in_, func, R=(), W=(), bias=None, scale=1.0, accum=None):
            kw = {}
            if bias is not None:
                kw["bias"] = bias
            if accum is not None:
                kw["accum_out"] = accum
            P.op("act", lambda e: e.activation(out, in_, func, scale=scale, **kw), reads=R, writes=W)

        def CP(out, in_, R=(), W=(), eng="dve"):
            if eng == "act":
                ACTF(out, in_, AF.Copy, R, W)
            else:
                P.op(eng, lambda e: e.tensor_copy(out, in_), reads=R, writes=W)

        def TT(out, in0, in1, op, R=(), W=(), eng="dve"):
            P.op(eng, lambda e: e.tensor_tensor(out, in0, in1, op), reads=R, writes=W)

        def TS(out, in0, s1, s2, op0, op1=None, R=(), W=(), eng="dve"):
            if op1 is None:
                P.op(eng, lambda e: e.tensor_scalar(out, in0, s1, s2, op0), reads=R, writes=W)
            else:
                P.op(eng, lambda e: e.tensor_scalar(out, in0, s1, s2, op0, op1), reads=R, writes=W)

        def STT(out, in0, scalar, in1, op0, op1, R=(), W=()):
            P.op("dve", lambda e: e.scalar_tensor_tensor(out=out, in0=in0, scalar=scalar, in1=in1, op0=op0, op1=op1), reads=R, writes=W)

        def RCP(out, in_, R=(), W=()):
            P.op("dve", lambda e: e.reciprocal(out, in_), reads=R, writes=W)

        def MSET(ap, val, W=(), eng="dve"):
            P.op(eng, lambda e: e.memset(ap, val), writes=W)

        def DMA(q, out, in_, R=(), W=(), slow=False):
            if slow:
                P.dma(q, lambda e: e.dma_start(out=out, in_=in_, allow_slow_non_contiguous=True), reads=R, writes=W)
            else:
                P.dma(q, lambda e: e.dma_start(out=out, in_=in_), reads=R, writes=W)

        cm = sb("cm_s", [128, 768])
        DMA("sp", cm[:], cm_d, W=["cm"])
        ident = cm[:, 0:128]
        ones32 = cm[:, 128:256]
        Um = cm[:, 256:384]
        USTm = cm[:, 384:512]
        MTm = cm[:, 512:640]
        SUm = cm[:, 640:768]
        ones_bf = sb("ones_bf", [128, 128], BF16)
        ident_bf = sb("ident_bf", [128, 128], BF16)
        CP(ones_bf[:], cm[:, 128:256], R=["cm"], W=["ones_bf"])
        CP(ident_bf[:], cm[:, 0:128], R=["cm"], W=["ident_bf"])
        ln1_s = sb("ln1_s", [128, KC])
        DMA("sp", ln1_s[:], ln1, W=["ln1"])
        kmb_s = sb("kmb_s", [128, 96])
        DMA("sp", kmb_s[:], kmb, W=["kmb"])
        eps_t = sb("eps_t", [128, 1])
        one_t = sb("one_t", [128, 1])
        MSET(eps_t[:], EPS, W=["eps"])
        MSET(one_t[:], 1.0, W=["one"])

        esX = ExitStack()
        sbx = lambda name, shape, dt=F32: esX.enter_context(nc.sbuf_tensor(name, list(shape), dt))
        xbf = sbx("xbf", [128, KC, NT], BF16)
        blocks = [(512 * i, 512) for i in range(8)] + [(VT, 128)]
        with ExitStack() as es0:
            stage = es0.enter_context(nc.sbuf_tensor("stage", [128, KC, 512], F32))
            sq = es0.enter_context(nc.sbuf_tensor("sq", [128, 2, 512], BF16))
            rb = es0.enter_context(nc.sbuf_tensor("rb", [128, 512], F32))
            for bi, (c0, n) in enumerate(blocks):
                for kc in range(KC):
                    DMA("sp" if kc % 2 == 0 else "act", stage[:, kc, 0:n], xT[kc * 128:(kc + 1) * 128, c0:c0 + n], W=[f"stage{kc}"])
                for kc in range(KC):
                    ACTF(sq[:, kc % 2, 0:n], stage[:, kc, 0:n], AF.Square, R=[f"stage{kc}"], W=[f"sq{kc % 2}"])
                    MM(psb[0][:, 0:n], ones_bf[:], sq[:, kc % 2, 0:n], start=(kc == 0), stop=(kc == KC - 1), R=[f"sq{kc % 2}", "ones_bf"], W=["ps0"])
                ACTF(rb[:, 0:n], psb[0][:, 0:n], AF.Sqrt, R=["eps"], W=["ps0", "rb"], bias=eps_t[:], scale=1.0 / D)
                RCP(rb[:, 0:n], rb[:, 0:n], W=["rb"])
                for kc in range(KC):
                    STT(xbf[:, kc, c0:c0 + n], stage[:, kc, 0:n], ln1_s[:, kc:kc + 1], rb[:, 0:n], ALU.mult, ALU.mult,
                        R=[f"stage{kc}", "rb", "ln1"], W=[f"xbf{bi}"])
        P.mark('phase0_done')
        P.fence()
        xbf_all = [f"xbf{bi}" for bi in range(len(blocks))]

        wt = [sbx(f"wt{i}", [128, KC, 128], BF16) for i in range(2)]
        wt_n = [0]

        def load_w(c0):
            i = wt_n[0] % 2
            wt_n[0] += 1
            DMA("pool", wt[i][:], w_in[c0 // 128], W=[f"wt{i}"])
            return i

        def gemm_fm(wi, t0, n, ps, pname):
            for kc in range(KC):
                MM(ps[:, 0:n], wt[wi][:, kc, :], xbf[:, kc, t0:t0 + n], start=(kc == 0), stop=(kc == KC - 1), R=[f"wt{wi}"] + xbf_all, W=[pname])

        sbias = sbx("sbias", [128, 3, 8])
        sself = sbx("sself", [1, 3, 8])
        with ExitStack() as esA:
            sba = lambda name, shape, dt=F32: esA.enter_context(nc.sbuf_tensor(name, list(shape), dt))
            rowt = sba("rowt", [1, 128])
            KT = sba("KT", [128, VT], BF16)
            VTt = sba("VTt", [128, VT], BF16)
            QT = sba("QT", [128, 2048], BF16)
            Vtm = sba("Vtm", [128, len(KB), 128], BF16)
            kvf = sba("kvf", [128, 2, 512], F32)
            Bm = sba("Bm", [128, 6, 128], F32)
            Erow = sba("Erow", [128, 384], F32)
            oh_s = sba("oh_s", [32, 3 * 384])
            negv_s = sba("negv_s", [128, 3 * 384])
            rbias_s = sba("rbias_s", [32, 8])
            rbb = sba("rbb", [32, 128])
            DMA("sp", oh_s[:], oh_d, W=["oh"])
            DMA("sp", negv_s[:], negv_d.partition_broadcast(128), W=["negv"])
            DMA("sp", rbias_s[:], rel_bias, W=["rbias"])
            ssc = sba("ssc", [128, 8, 128], F32)
            PT = [sba(f"PT{i}", [128, 2, 128], BF16) for i in range(4)]
            attb = sba("attb", [128, 512], BF16)
            dent = sba("dent", [128, 512], F32)
            scale = 128 ** -0.5
            unit_n = 0
            for h in range(8 if stop_after != "skipatt" else 0):
                CP(rbb[:], rbias_s[:, h:h + 1].to_broadcast([32, 128]), R=["rbias"], W=["rbb"])
                for b in range(3):
                    MM(psb[1][:, 0:384], rbb[:], oh_s[:, b * 384:(b + 1) * 384], R=["rbb", "oh"], W=["ps1"])
                    TT(Erow[:], psb[1][:, 0:384], negv_s[:, b * 384:(b + 1) * 384], ALU.add, R=["negv"], W=["ps1", "Erow"])
                    DMA("sp", Yscr, Erow[:], R=["Erow"], W=["Yscr"])
                    for cp, base in ((0, 128), (1, 256)):
                        DMA("sp", Bm[:, 2 * b + cp, :], bass.AP(Yscr.tensor, base, [[383, 128], [1, 128]]), R=["Yscr"], W=["Bm"])
                for b in range(3):
                    CP(sbias[:, b, h:h + 1], Bm[:, 2 * b + 1, 0:1], R=["Bm"], W=["sbias"])
                    CP(sself[0:1, b, h:h + 1], Bm[0:1, 2 * b, 0:1], R=["Bm"], W=["sself"])
                for which, c0, dst, t_lo in (("k", 1024 + 128 * h, KT, 0), ("v", 2048 + 128 * h, VTt, 0), ("q", 128 * h, QT, 2048)):
                    t_first = t_lo // 512 if which != "q" else 5
                    wi = load_w(c0)
                    for tb in range(t_first, 8):
                        pb = 2 + (tb % 2)
                        gemm_fm(wi, 512 * tb, 512, psb[pb], f"ps{pb}")
                        dsl = dst[:, 512 * tb - t_lo:512 * tb - t_lo + 512]
                        if which != "q" and tb >= 4:
                            kb_ = tb % 2
                            CP(kvf[:, kb_, :], psb[pb][:, :], W=[f"ps{pb}", f"kvf{kb_}"])
                            CP(dsl, kvf[:, kb_, :], R=[f"kvf{kb_}"], W=[which + "T"], eng="act")
                            od = o_wk if which == "k" else o_wv
                            DMA("sp", od[h, :, 512 * (tb - 4):512 * (tb - 4) + 512], kvf[:, kb_, :], R=[f"kvf{kb_}"], W=["o_w"])
                        else:
                            CP(dsl, psb[pb][:, :], W=[f"ps{pb}", which + "T"], eng="act")
                    for kc in range(KC):
                        MM(psb[0][0:1, 0:128], xbf[:, kc, VT:VT + 1], wt[wi][:, kc, :], start=(kc == 0), stop=(kc == KC - 1), R=[f"wt{wi}"] + xbf_all, W=["ps0"])
                    CP(rowt[:], psb[0][0:1, 0:128], W=["ps0", "rowt"])
                    DMA("sp", rowD[{"q": 0, "k": 1, "v": 2}[which]:{"q": 0, "k": 1, "v": 2}[which] + 1, 128 * h:128 * h + 128], rowt[:], R=["rowt"], W=["rowD"])
                    if which != "q":
                        gemm_fm(wi, VT, 128, psb[2], "ps2")
                        CP(kvf[:, 0, 0:128], psb[2][:, 0:128], W=["ps2", "kvf0"])
                        od = o_swk if which == "k" else o_swv
                        DMA("sp", od[2047:2048, 128 * h:128 * h + 128].rearrange("a d -> d a"), kvf[:, 0, 0:1], R=["kvf0"], W=["o_sw"])
                for (b, r, n), idx in KB.items():
                    dil = DILS[b]
                    t0 = dil * 128 * n + r
                    pb = 4 + (idx % 2)
                    pv = psb[pb][:, :].bitcast(BF16)[:, 0:128]
                    TRP(pv, VTt[:, t0:t0 + 127 * dil + 1:dil], ident_bf[:], R=["vT", "ident_bf"], W=[f"ps{pb}"])
                    CP(Vtm[:, idx, :], pv, W=[f"ps{pb}", "Vtm"], eng=("act" if idx % 2 == 0 else "dve"))
                for m, (base, units) in enumerate(RANGES):
                    def stageA(u, ui):
                        b, dil, r, n, nq = u["b"], u["dil"], u["r"], u["n"], u["nq"]
                        sps = psb[2 + ui]
                        sn = f"ps{2 + ui}"
                        q0 = u["q0"] - 2048
                        qsl = slice(q0, q0 + (nq - 1) * dil + 1, dil)
                        for cp, kn in ((0, n), (1, n - 1)):
                            k0 = dil * 128 * kn + r
                            MM(sps[:, cp * 128:cp * 128 + nq], KT[:, k0:k0 + 127 * dil + 1:dil], QT[:, qsl], R=["kT", "qT"], W=[sn])
                        for cp in (0, 1):
                            STT(ssc[:, 2 * ui + cp, 0:nq], sps[:, cp * 128:cp * 128 + nq], scale, Bm[:, 2 * b + cp, u["msl"][0]:u["msl"][1]], ALU.mult, ALU.add,
                                R=["Bm"], W=[sn, f"ssc{ui}_{cp}"])
                        for cp, kn in ((0, n), (1, n - 1)):
                            kbi = {0: kn, 1: 32 + r * 8 + kn, 2: 64 + r * 2 + kn}[b]
                            ACTF(PT[ui][:, cp, 0:nq], ssc[:, 2 * ui + cp, 0:nq], AF.Exp, R=[f"ssc{ui}_{cp}", "kmb"], W=[f"PT{ui}_{cp}"], bias=kmb_s[:, kbi:kbi + 1])

                    def stageB(u, ui, first):
                        b, r, n, nq = u["b"], u["r"], u["n"], u["nq"]
                        osl = slice(u["ocol"], u["ocol"] + (nq - 1) * u["ostep"] + 1, u["ostep"])
                        for cp, kn in ((0, n), (1, n - 1)):
                            vi = KB[(b, r, kn)]
                            st = first and cp == 0
                            MM(psb[6][:, osl], Vtm[:, vi, :], PT[ui][:, cp, 0:nq], start=st, stop=False, R=["Vtm", f"PT{ui}_{cp}"], W=["ps6"], skip=True)
                            MM(psb[7][:, osl], ones_bf[:], PT[ui][:, cp, 0:nq], start=st, stop=False, R=["ones_bf", f"PT{ui}_{cp}"], W=["ps7"], skip=True)

                    DEPTH = 3
                    for i_, u in enumerate(units):
                        stageA(u, i_ % 4)
                        if i_ >= DEPTH:
                            stageB(units[i_ - DEPTH], (i_ - DEPTH) % 4, i_ - DEPTH == 0)
                    for i_ in range(max(len(units) - DEPTH, 0), len(units)):
                        stageB(units[i_], i_ % 4, i_ == 0)
                    TS(dent[:], psb[7][:, :], 1e-30, None, ALU.max, W=["ps7", "dent"])
                    RCP(dent[:], dent[:], W=["dent"])
                    TT(attb[:], psb[6][:, :], dent[:], ALU.mult, R=["dent"], W=["ps6", "attb"])
                    lo = max(base, P2_0)
                    DMA("sp", mixD[128 * h:128 * h + 128, lo - P2_0:base + 512 - P2_0], attb[:, lo - base:512], R=["attb"], W=["mixD"])
        P.fence()

        P.mark('att_done')
        SCOL = P2N - 128
        zt = sbx("zt", [128, 128], BF16)
        MSET(zt[:], 0.0, W=["zt"])
        for h in range(8):
            DMA("sp", mixD[128 * h:128 * h + 128, P2N - 128:P2N], zt[:], R=["zt"], W=["mixD"])

        with ExitStack() as esS:
            sbs = lambda name, shape, dt=F32: esS.enter_context(nc.sbuf_tensor(name, list(shape), dt))
            qkv = sbs("qkvrow", [1, 3072])
            qb = sbs("qb", [128, 1024])
            Kc = sbs("Kc", [128, 1024])
            Vc = sbs("Vc", [128, 1024])
            s8 = sbs("s8", [128, 8])
            p8 = sbs("p8", [128, 8])
            r8 = sbs("r8", [1, 8, 4])
            prow = sbs("prow", [1, 1024])
            arow = sbs("arow", [1, 1024])
            acol = sbs("acol", [128, 8], BF16)
            DMA("sp", qkv[:], rowD.rearrange("a (o c) -> o (a c)", o=1), R=["rowD"], W=["qkv"])
            qrow, krow, vrow = qkv[0:1, 0:1024], qkv[0:1, 1024:2048], qkv[0:1, 2048:3072]
            for hf in range(2):
                MM(psb[hf][:, 0:512], ones32[0:1, 0:128], qkv[0:1, 512 * hf:512 * hf + 512], R=["cm", "qkv"], W=[f"ps{hf}"])
                CP(qb[:, 512 * hf:512 * hf + 512], psb[hf][:, 0:512], W=[f"ps{hf}", "qb"])
            for b, dil in enumerate(DILS):
                off = (2048 - 128 * dil) * 1024
                DMA("sp", Kc[:], bass.AP(cache_k.tensor, off, [[dil * 1024, 128], [1, 1024]]), W=["Kc"])
                DMA("act", Vc[:], bass.AP(cache_v.tensor, off, [[dil * 1024, 128], [1, 1024]]), W=["Vc"])
                TT(Kc[:], Kc[:], qb[:], ALU.mult, R=["qb"], W=["Kc"])
                P.op("dve", lambda e: e.tensor_reduce(s8[:], Kc[:].rearrange("p (h d) -> p h d", h=8), AX.X, ALU.add), reads=["Kc"], writes=["s8"])
                STT(s8[:], s8[:], scale, sbias[:, b, :], ALU.mult, ALU.add, R=["sbias"], W=["s8"])
                ACTF(p8[:], s8[:], AF.Exp, R=["s8"], W=["p8"])
                TT(Vc[:].rearrange("p (h d) -> p h d", h=8), Vc[:].rearrange("p (h d) -> p h d", h=8), p8[:].unsqueeze(2).to_broadcast([128, 8, 128]), ALU.mult, R=["p8"], W=["Vc"])
                for hf in range(2):
                    MM(psb[2 + hf][0:1, 0:512], ones32[:, 0:1], Vc[:, 512 * hf:512 * hf + 512], start=(b == 0), stop=(b == 2), R=["cm", "Vc"], W=[f"ps{2 + hf}"])
                MM(psb[4][0:1, 0:8], ones32[:, 0:1], p8[:], start=(b == 0), stop=(b == 2), R=["cm", "p8"], W=["ps4"])
            TT(prow[:], qrow, krow, ALU.mult, R=["qkv"], W=["prow"])
            P.op("dve", lambda e: e.tensor_reduce(r8[0:1, :, 0], prow[:].rearrange("p (h d) -> p h d", h=8), AX.X, ALU.add), reads=["prow"], writes=["r8"])
            for b in range(3):
                STT(r8[0:1, :, 1 + b], r8[0:1, :, 0], scale, sself[0:1, b, :], ALU.mult, ALU.add, R=["sself"], W=["r8"])
            ACTF(r8[0:1, :, 1:4], r8[0:1, :, 1:4], AF.Exp, W=["r8"])
            P.op("dve", lambda e: e.tensor_reduce(r8[0:1, :, 0], r8[0:1, :, 1:4], AX.X, ALU.add), writes=["r8"])
            TT(prow[:].rearrange("p (h d) -> p h d", h=8), vrow.rearrange("p (h d) -> p h d", h=8), r8[0:1, :, 0:1].to_broadcast([1, 8, 128]), ALU.mult, R=["qkv", "r8"], W=["prow"])
            for hf in range(2):
                TT(arow[0:1, 512 * hf:512 * hf + 512], psb[2 + hf][0:1, 0:512], prow[0:1, 512 * hf:512 * hf + 512], ALU.add, R=["prow"], W=[f"ps{2 + hf}", "arow"])
            TT(r8[0:1, :, 1], psb[4][0:1, 0:8], r8[0:1, :, 0], ALU.add, W=["ps4", "r8"])
            RCP(r8[0:1, :, 1], r8[0:1, :, 1], W=["r8"])
            TT(arow[:].rearrange("p (h d) -> p h d", h=8), arow[:].rearrange("p (h d) -> p h d", h=8), r8[0:1, :, 1:2].to_broadcast([1, 8, 128]), ALU.mult, R=["r8"], W=["arow"])
            for h in range(8):
                MM(psb[5][:, h:h + 1], arow[0:1, 128 * h:128 * h + 128], ones32[0:1, 0:1], R=["arow", "cm"], W=["ps5"], skip=True)
            CP(acol[:], psb[5][:, 0:8], W=["ps5", "acol"])
            for h in range(8):
                DMA("sp", mixD[128 * h:128 * h + 128, SCOL:SCOL + 1], acol[:, h:h + 1], R=["acol"], W=["mixD"], slow=True)
        P.fence()
        P.mark('sample_att_done')
        dnc_s = sbx("dnc_s", [128, 4], F32)
        for cb in range(16):
            wi = load_w(3072 + 128 * cb)
            for kc in range(KC):
                MM(psb[2][:, 0:4], wt[wi][:, kc, :], xbf[:, kc, VT - 3:VT + 1], start=(kc == 0), stop=(kc == KC - 1), R=[f"wt{wi}"] + xbf_all, W=["ps2"])
            CP(dnc_s[:], psb[2][:, 0:4], W=["ps2", "dnc_s"])
            DMA("sp", o_dnc[cb], dnc_s[:], R=["dnc_s"], W=["o_dnc"])
            DMA("sp", o_sdc[2:3, 128 * cb:128 * cb + 128].rearrange("a d -> d a"), dnc_s[:, 3:4], R=["dnc_s"], W=["o_sdc"])

        copy_chunks = []
        for (src, dst) in ((cache_k, o_swk), (cache_v, o_swv)):
            for i in range(64):
                w_ = 256 if i < 63 else 16376 - 63 * 256
                copy_chunks.append((bass.AP(dst.tensor, 256 * i, [[16376, 128], [1, w_]]), bass.AP(src.tensor, 1024 + 256 * i, [[16376, 128], [1, w_]])))
        DMA("sp", o_sdc[0:2, :], st_dnc[1:3, :], W=["o_sdc"])
        DMA("sp", o_sfc[0:1, :], st_ffc[1:2, :], W=["o_sfc"])

        if stop_after != "att":
            with ExitStack() as esD:
                NG = NTILES * 8
                sbd = lambda name, shape, dt=F32: esD.enter_context(nc.sbuf_tensor(name, list(shape), dt))
                nw_s = sbd("nw_s", [128, 128])
                gmask_s = sbd("gmask_s", [128, NTILES])
                chm_s = sbd("chm_s", [128, 2])
                cw_s = sbd("cw_s", [128, 16, 4])
                stf_s = sbd("stf_s", [128, 16, 3])
                beta = sbd("beta", [128, NTILES, 8])
                g = sbd("g", [128, NTILES, 8])
                gcol = sbd("gcol", [128, NG])
                egc = sbd("egc", [128, NG])
                negegc = sbd("negegc", [128, NG])
                egrem = sbd("egrem", [128, NG])
                egsum = sbd("egsum", [128, 2 * NG])
                esT = ExitStack()
                sbt = lambda name, shape, dt=F32: esT.enter_context(nc.sbuf_tensor(name, list(shape), dt))
                gsel = sbt("gsel", [128, NTILES, 2, 8])
                wba = sbt("wba", [128, KC, 16], BF16)
                DMA("pool", wba[:], w_in[48][:, :, 0:16], W=["wba"])
                bbba = sbt("bbba", [128, NTILES, 16])
                for t in range(NTILES):
                    bk, bn = (psb[0], "ps0") if t < 32 else (psb[1], "ps1")
                    col = 16 * (t % 32)
                    for kc in range(KC):
                        MM(bk[:, col:col + 16], xbf[:, kc, 128 * t:128 * t + 128], wba[:, kc, :], start=(kc == 0), stop=(kc == KC - 1), R=["wba"] + xbf_all, W=[bn], skip=True)
                CP(bbba[:, 0:32, :], psb[0][:, :].rearrange("p (t c) -> p t c", c=16), W=["ps0", "bbba"])
                CP(bbba[:, 32, :], psb[1][:, 0:16], W=["ps1", "bbba"])
                dtb_s = sbt("dtb_s", [128, 8])
                alog_s = sbt("alog_s", [128, 8])
                DMA("sp", dtb_s[:], dtb_d.partition_broadcast(128), W=["dtb"])
                DMA("sp", alog_s[:], alog_d.partition_broadcast(128), W=["alog"])
                DMA("sp", nw_s[:], nw_d.partition_broadcast(128), W=["nw"])
                DMA("sp", gmask_s[:], gmask_d, W=["gmask"])
                DMA("sp", chm_s[:], chm_d, W=["chm"])
                DMA("sp", cw_s[:].rearrange("p a b -> p (a b)"), cw_d, W=["cw"])
                DMA("sp", stf_s[:].rearrange("p a b -> p (a b)"), stf_d, W=["stf"])
                nexpA = sbt("nexpA", [128, 8])
                ACTF(nexpA[:], alog_s[:], AF.Exp, R=["alog"], W=["nexpA"])
                TS(nexpA[:], nexpA[:], -1.0, None, ALU.mult, W=["nexpA"])
                tma = sbt("tma", [128, NTILES, 8])
                tmb = sbt("tmb", [128, NTILES, 8])
                ACTF(beta[:], bbba[:, :, 0:8], AF.Sigmoid, R=["bbba"], W=["beta"])
                TT(tma[:], bbba[:, :, 8:16], dtb_s[:].unsqueeze(1).to_broadcast([128, NTILES, 8]), ALU.add, R=["bbba", "dtb"], W=["tma"])
                ACTF(tmb[:], tma[:], AF.Abs, R=["tma"], W=["tmb"])
                ACTF(tmb[:], tmb[:], AF.Exp, W=["tmb"], scale=-1.0)
                ACTF(tmb[:], tmb[:], AF.Ln, R=["one"], W=["tmb"], bias=one_t[:])
                TS(tma[:], tma[:], 0.0, None, ALU.max, W=["tma"])
                TT(tma[:], tma[:], tmb[:], ALU.add, R=["tmb"], W=["tma"])
                TT(g[:], tma[:], nexpA[:].unsqueeze(1).to_broadcast([128, NTILES, 8]), ALU.mult, R=["tma", "nexpA"], W=["g"])
                TT(g[:], g[:], gmask_s[:].unsqueeze(2).to_broadcast([128, NTILES, 8]), ALU.mult, R=["gmask"], W=["g"])
                for c in range(2):
                    TS(gsel[:, :, c, :], g[:], chm_s[:, c:c + 1], None, ALU.mult, R=["g", "chm"], W=["gsel"])
                gflat = g[:].rearrange("p t h -> p (t h)")
                MM(psb[2][:, 0:NG], Um, gflat, R=["cm", "g"], W=["ps2"])
                CP(gcol[:], psb[2][:, 0:NG], W=["ps2", "gcol"])
                ACTF(egc[:], gcol[:], AF.Exp, R=["gcol"], W=["egc"])
                TS(negegc[:], egc[:], -1.0, None, ALU.mult, R=["egc"], W=["negegc"])
                MM(psb[3][:, 0:NG], USTm, gflat, R=["cm", "g"], W=["ps3"])
                ACTF(egrem[:], psb[3][:, 0:NG], AF.Exp, W=["ps3", "egrem"])
                gsf = gsel[:].rearrange("p t c h -> p (t c h)")
                MM(psb[2][:, 0:512], ones32, gsf[:, 0:512], R=["cm", "gsel"], W=["ps2"])
                ACTF(egsum[:, 0:512], psb[2][:, 0:512], AF.Exp, W=["ps2", "egsum"])
                MM(psb[3][:, 0:16], ones32, gsf[:, 512:528], R=["cm", "gsel"], W=["ps3"])
                ACTF(egsum[:, 512:528], psb[3][:, 0:16], AF.Exp, W=["ps3", "egsum"])

                esT.close()
                P.fence()
                QnT = sbd("QnT", [128, NT], BF16)
                KnT = sbd("KnT", [128, NT], BF16)
                VT2 = [sbd(f"VT2_{i}", [128, NT], BF16) for i in range(2)]
                wz = wt
                uni = sbd("uni", [128, 1030])
                class Sub:
                    def __init__(self, t, c0, n):
                        self.t, self.c0, self.n = t, c0, n

                    def __getitem__(self, key):
                        rows, cols = key
                        a = (cols.start or 0) + self.c0
                        b = (cols.stop if cols.stop is not None else self.n) + self.c0
                        return self.t[rows, a:b]
                pre = Sub(uni, 0, 515)
                yb = Sub(uni, 515, 512)
                sl = yb
                gbc2 = [sbd(f"gbc{i}", [128, 128]) for i in range(2)]
                Dm = [sbd(f"Dm{i}", [128, 128]) for i in range(2)]
                decT = Dm
                Nm = [sbd(f"Nm{i}", [128, 256], BF16) for i in range(2)]
                uni_b = uni[:, :].bitcast(BF16)
                PP = [[Sub(uni_b, 256 * (2 * i + j), 256) for j in range(2)] for i in range(2)]
                Xb = [[sbd(f"X{i}_{j}", [128, 128], BF16) for j in range(2)] for i in range(2)]
                Xbf = [[sbd(f"Xbf{p}{i}", [128, 128], BF16) for i in range(2)] for p in range(2)]
                attnT = [[sbd(f"attnT{p}{i}", [128, 128], BF16) for i in range(2)] for p in range(2)]
                Kntm = sbd("Kntm", [128, 128], BF16)
                Vtm2 = [[sbd(f"Vtm2_{p}{i}", [128, 128], BF16) for i in range(2)] for p in range(2)]
                kdz = [[[sbd(f"kdz{p}{i}_{c}", [128, 128], BF16) for c in range(2)] for i in range(2)] for p in range(2)]
                Rb = [sbd(f"Rb{i}", [128, 128], BF16) for i in range(2)]
                vn = [sbd(f"vn{i}", [128, 128], BF16) for i in range(2)]
                otmp = [sbd(f"otmp{i}", [128, 128]) for i in range(2)]
                ob = otmp
                S32 = [sbd(f"S32_{i}", [128, 128]) for i in range(2)]
                Sbf = [sbd(f"Sbf{i}", [128, 128], BF16) for i in range(2)]
                ssq2 = [sbd(f"ssq{i}", [128, 2]) for i in range(2)]
                szb2 = [sbd(f"szb{i}", [128, 128], BF16) for i in range(2)]
                onb2 = [sbd(f"onb{i}", [128, 128]) for i in range(2)]
                dnT2 = [sbd(f"dnT{i}", [128, 128], BF16) for i in range(2)]
                for i in range(2):
                    MSET(Rb[i][:], 0.0, W=[f"Rb{i}"])
                    MSET(vn[i][:], 0.0, W=[f"vn{i}"])
                    for c in range(2):
                        for p in range(2):
                            MSET(kdz[p][i][c][:], 0.0, W=[f"kdz{p}{i}"])

                for hq in range(4 if stop_after != "dnpre" else 0):
                    heads = (2 * hq, 2 * hq + 1)
                    specs = [("q", 3072 + 128 * hq, hq), ("k", 3584 + 128 * hq, 4 + hq), ("v0", 4096 + 128 * heads[0], 8 + heads[0]), ("v1", 4096 + 128 * heads[1], 8 + heads[1])]
                    blocks256 = [(256 * i, 256) for i in range(16)] + [(VT, 128)]

                    def proj(which, c0w, cb, slot):
                        pre_ = Sub(uni, slot * 515, 259)
                        yb_ = Sub(uni, slot * 515 + 259, 256)
                        pn, yn, sn_ = f"pre{slot}", f"yb{slot}", f"sqb{slot}"
                        pg, pgn = psb[slot], f"ps{slot}"
                        pq, pqn = psb[3 + slot], f"ps{3 + slot}"
                        sq_ = Sub(Nm[slot], 0, 256)
                        wi = load_w(c0w)
                        yield
                        MSET(pre_[:, 0:3], 0.0, W=[pn])
                        yield
                        for bi, (c0, n) in enumerate(blocks256):
                            for kc in range(KC):
                                MM(pg[:, 0:n], wt[wi][:, kc, :], xbf[:, kc, c0:c0 + n], start=(kc == 0), stop=(kc == KC - 1), R=[f"wt{wi}"] + xbf_all, W=[pgn])
                                if kc % 4 == 3:
                                    yield
                            if bi == 16:
                                CP(pre_[:, 0:3], stf_s[:, cb, :], R=["stf"], W=[pn])
                                yield
                            CP(pre_[:, 3:3 + n], pg[:, 0:n], W=[pgn, pn], eng="act")
                            yield
                            TS(yb_[:, 0:n], pre_[:, 0:n], cw_s[:, cb, 0:1], None, ALU.mult, R=[pn, "cw"], W=[yn])
                            yield
                            for j in range(1, 4):
                                STT(yb_[:, 0:n], pre_[:, j:j + n], cw_s[:, cb, j:j + 1], yb_[:, 0:n], ALU.mult, ALU.add, R=[pn, "cw"], W=[yn])
                                yield
                            if bi == 16:
                                MSET(yb_[:, 1:128], 0.0, W=[yn])
                                yield
                            elif bi < 15:
                                CP(pre_[:, 0:3], pre_[:, n:n + 3], W=[pn])
                                yield
                            if which in ("v0", "v1"):
                                ACTF(VT2[int(which[1])][:, c0:c0 + n], yb_[:, 0:n], AF.Silu, R=[yn], W=["VT2_" + which[1]])
                                yield
                            else:
                                ACTF(yb_[:, 0:n], yb_[:, 0:n], AF.Silu, W=[yn])
                                yield
                                ACTF(sq_[:, 0:n], yb_[:, 0:n], AF.Square, R=[yn], W=[sn_])
                                yield
                                MM(pq[:, 0:n], ones_bf[:], sq_[:, 0:n], R=[sn_, "ones_bf"], W=[pqn])
                                yield
                                ACTF(pre_[:, 3:3 + n], pq[:, 0:n], AF.Sqrt, R=["eps"], W=[pqn, pn], bias=eps_t[:])
                                yield
                                RCP(pre_[:, 3:3 + n], pre_[:, 3:3 + n], W=[pn])
                                yield
                                if which == "q":
                                    STT(QnT[:, c0:c0 + n], yb_[:, 0:n], 128 ** -0.5, pre_[:, 3:3 + n], ALU.mult, ALU.mult, R=[yn, pn], W=["QnT"])
                                else:
                                    TT(KnT[:, c0:c0 + n], yb_[:, 0:n], pre_[:, 3:3 + n], ALU.mult, R=[yn, pn], W=["KnT"])
                                yield

                    for pair in ((specs[0], specs[1]), (specs[2], specs[3])):
                        for _ in itertools.zip_longest(proj(*pair[0], 0), proj(*pair[1], 1)):
                            pass
                    for hh in range(2):
                        DMA("pool", wz[hh][:], w_in[40 + heads[hh]], W=[f"wt{hh}"])
                        MSET(S32[hh][:], 0.0, W=[f"S32_{hh}"])
                        MSET(Sbf[hh][:], 0.0, W=[f"Sbf{hh}"])
                    P.mark(f'dn_proj_done_{hq}')
                    P.fence()
                    p7b = psb[7][:, :].bitcast(BF16)

                    def shared(t):
                        tp = t % 2
                        tsl = slice(128 * t, 128 * t + 128)
                        MM(psb[7][:, 0:128], KnT[:, tsl], KnT[:, tsl], R=["KnT"], W=["ps7"], skip=True)
                        yield
                        MM(psb[7][:, 128:256], KnT[:, tsl], QnT[:, tsl], R=["KnT", "QnT"], W=["ps7"], skip=True)
                        yield
                        TRP(p7b[:, 512:640], KnT[:, tsl], ident_bf[:], R=["KnT", "ident_bf"], W=["ps7"])
                        yield
                        for hh in range(2):
                            TRP(p7b[:, 640 + 128 * hh:768 + 128 * hh], VT2[hh][:, tsl], ident_bf[:], R=[f"VT2_{hh}", "ident_bf"], W=["ps7"])
                            yield
                        CP(Kntm[:], p7b[:, 512:640], W=["ps7", "Kntm"])
                        yield
                        for hh in range(2):
                            CP(Vtm2[tp][hh][:], p7b[:, 640 + 128 * hh:768 + 128 * hh], W=["ps7", f"Vtm2_{tp}{hh}"])
                            yield

                    def prep(t, hh):
                        tp = t % 2
                        h = heads[hh]
                        gbc = gbc2[hh]
                        col = t * 8 + h
                        bA, nA = psb[3 + 2 * hh], f"ps{3 + 2 * hh}"
                        bB, nB = psb[4 + 2 * hh], f"ps{4 + 2 * hh}"
                        aT, aTn = attnT[tp][hh], f"attnT{tp}{hh}"
                        CP(gbc[:], g[:, t, h:h + 1].to_broadcast([128, 128]), R=["g"], W=[f"gbc{hh}"], eng="pool")
                        yield
                        MM(bB[:, 0:128], gbc[:], Um, R=[f"gbc{hh}", "cm"], W=[nB])
                        yield
                        STT(Dm[hh][:], bB[:, 0:128], gcol[:, col:col + 1], MTm, ALU.subtract, ALU.add, R=["gcol", "cm"], W=[nB, f"Dm{hh}"])
                        yield
                        ACTF(Dm[hh][:], Dm[hh][:], AF.Exp, W=[f"Dm{hh}"])
                        yield
                        TT(Nm[hh][:, 128:256], psb[7][:, 0:128], Dm[hh][:], ALU.mult, R=[f"Dm{hh}"], W=["ps7", f"Nm{hh}"])
                        yield
                        STT(Nm[hh][:, 0:128], Nm[hh][:, 128:256], beta[:, t, h:h + 1], SUm, ALU.mult, ALU.mult, R=["beta", "cm"], W=[f"Nm{hh}"])
                        yield
                        TT(aT[:], psb[7][:, 128:256], Dm[hh][:], ALU.mult, R=[f"Dm{hh}"], W=["ps7", aTn])
                        yield
                        bAb = bA[:, :].bitcast(BF16)
                        TRP(bAb[:, 512:640], Nm[hh][:, 0:128], ident_bf[:], R=[f"Nm{hh}", "ident_bf"], W=[nA])
                        yield
                        CP(Nm[hh][:, 128:256], bAb[:, 512:640], W=[nA, f"Nm{hh}"], eng="act")
                        yield
                        TT(Xb[hh][0][:], ident, Nm[hh][:, 0:128], ALU.subtract, R=["cm", f"Nm{hh}"], W=[f"X{hh}_0"], eng="pool")
                        yield
                        cur = Nm[hh]
                        curn = f"Nm{hh}"
                        for k in range(5):
                            nxt = PP[hh][k % 2]
                            nxtn = f"PP{hh}_{k % 2}"
                            if k < 4:
                                MM(bA[:, 0:128], cur[:, 128:256], cur[:, 0:128], R=[curn], W=[nA], skip=True)
                                yield
                                MM(bA[:, 128:256], cur[:, 0:128], cur[:, 128:256], R=[curn], W=[nA], skip=True)
                                yield
                                CP(nxt[:, 0:256], bA[:, 0:256], W=[nA, nxtn], eng="act")
                                yield
                            else:
                                MM(bA[:, 128:256], cur[:, 0:128], cur[:, 128:256], R=[curn], W=[nA], skip=True)
                                yield
                                CP(nxt[:, 128:256], bA[:, 128:256], W=[nA, nxtn], eng="act")
                                yield
                            xo = Xb[hh][k % 2]
                            xn_, xnn = (Xb[hh][(k + 1) % 2], f"X{hh}_{(k + 1) % 2}") if k < 4 else (Xbf[tp][hh], f"Xbf{tp}{hh}")
                            MM(bB[:, 0:128], nxt[:, 128:256], xo[:], R=[nxtn, f"X{hh}_{k % 2}"], W=[nB])
                            yield
                            TT(xn_[:], bB[:, 0:128], xo[:], ALU.add, R=[f"X{hh}_{k % 2}"], W=[nB, xnn])
                            yield
                            cur, curn = nxt, nxtn
                        for c in range(2):
                            rs = slice(64 * c, 64 * c + 64)
                            TS(kdz[tp][hh][c][rs, :], Kntm[rs, :], egrem[rs, col:col + 1], None, ALU.mult, R=["Kntm", "egrem"], W=[f"kdz{tp}{hh}"])
                            yield

                    def chain(t, hh):
                        tp = t % 2
                        tsl = slice(128 * t, 128 * t + 128)
                        h = heads[hh]
                        ssq, szb, onb, dnT = ssq2[hh], szb2[hh], onb2[hh], dnT2[hh]
                        pc, pn_ = psb[hh], f"ps{hh}"
                        col = t * 8 + h
                        if t == 32:
                            DMA("sp", o_pdr[h], S32[hh][:], R=[f"S32_{hh}"], W=["o_pdr"])
                            yield
                            DMA("sp", S32[hh][:], st_rec[h], W=[f"S32_{hh}"])
                            yield
                            CP(Sbf[hh][:], S32[hh][:], R=[f"S32_{hh}"], W=[f"Sbf{hh}"], eng="act")
                            yield
                        for c in range(2):
                            rs = slice(64 * c, 64 * c + 64)
                            MM(pc[:, 0:128], KnT[:, tsl], Sbf[hh][:], R=["KnT", f"Sbf{hh}"], W=[pn_], skip=True)
                            yield
                            MM(pc[:, 128:256], QnT[:, tsl], Sbf[hh][:], R=["QnT", f"Sbf{hh}"], W=[pn_], skip=True)
                            yield
                            STT(Rb[hh][rs, :], pc[rs, 0:128], negegc[rs, col:col + 1], Vtm2[tp][hh][rs, :], ALU.mult, ALU.add, R=["negegc", f"Vtm2_{tp}{hh}"], W=[pn_, f"Rb{hh}"])
                            yield
                            TS(otmp[hh][rs, :], pc[rs, 128:256], egc[rs, col:col + 1], None, ALU.mult, R=["egc"], W=[pn_, f"otmp{hh}"])
                            yield
                            MM(pc[:, 256:384], Xbf[tp][hh][:], Rb[hh][:], R=[f"Xbf{tp}{hh}", f"Rb{hh}"], W=[pn_], skip=True)
                            yield
                            TS(vn[hh][rs, :], pc[rs, 256:384], beta[rs, t, h:h + 1], None, ALU.mult, R=["beta"], W=[pn_, f"vn{hh}"])
                            yield
                            MM(pc[:, 384:512], kdz[tp][hh][c][:], vn[hh][:], R=[f"kdz{tp}{hh}", f"vn{hh}"], W=[pn_], skip=True)
                            yield
                            ec = t * 16 + c * 8 + h
                            STT(S32[hh][:], S32[hh][:], egsum[:, ec:ec + 1], pc[:, 384:512], ALU.mult, ALU.add, R=["egsum"], W=[pn_, f"S32_{hh}"])
                            yield
                            CP(Sbf[hh][:], S32[hh][:], R=[f"S32_{hh}"], W=[f"Sbf{hh}"], eng="act")
                            yield
                        if t >= 23:
                            MM(pc[:, 0:128], attnT[tp][hh][:], vn[hh][:], R=[f"attnT{tp}{hh}", f"vn{hh}"], W=[pn_], skip=True)
                            yield
                            TT(otmp[hh][:], pc[:, 0:128], otmp[hh][:], ALU.add, W=[pn_, f"otmp{hh}"])
                            yield
                            for kc in range(KC):
                                MM(pc[:, 128:256], xbf[:, kc, tsl], wz[hh][:, kc, :], start=(kc == 0), stop=(kc == KC - 1), R=[f"wt{hh}"] + xbf_all, W=[pn_], skip=True)
                                if kc % 4 == 3:
                                    yield
                            ACTF(szb[:], pc[:, 128:256], AF.Silu, W=[pn_, f"szb{hh}"])
                            yield
                            MSET(ssq[:, 0:1], 0.0, W=[f"ssq{hh}"], eng="pool")
                            yield
                            ACTF(onb[:], otmp[hh][:], AF.Square, R=[f"otmp{hh}"], W=[f"onb{hh}", f"ssq{hh}"], accum=ssq[:, 0:1])
                            yield
                            ACTF(ssq[:, 1:2], ssq[:, 0:1], AF.Sqrt, R=["eps"], W=[f"ssq{hh}"], bias=eps_t[:], scale=1.0 / 128)
                            yield
                            RCP(ssq[:, 1:2], ssq[:, 1:2], W=[f"ssq{hh}"])
                            yield
                            STT(onb[:], otmp[hh][:], ssq[:, 1:2], nw_s[:], ALU.mult, ALU.mult, R=[f"otmp{hh}", f"ssq{hh}", "nw"], W=[f"onb{hh}"])
                            yield
                            TT(onb[:], onb[:], szb[:], ALU.mult, R=[f"szb{hh}"], W=[f"onb{hh}"], eng="pool")
                            yield
                            TRP(pc[:, 256:384], onb[:], ident, R=[f"onb{hh}", "cm"], W=[pn_])
                            yield
                            CP(dnT[:], pc[:, 256:384], W=[pn_, f"dnT{hh}"])
                            yield
                            DMA("sp", mixD[1024 + 128 * h:1024 + 128 * h + 128, 128 * t - P2_0:128 * t - P2_0 + 128], dnT[:], R=[f"dnT{hh}"], W=["mixD"])
                            yield

                    for _ in shared(0):
                        pass
                    for step in range(-1, NTILES):
                        if copy_chunks:
                            da_, sa_ = copy_chunks.pop()
                            DMA("sp", da_, sa_, W=["o_sw"])
                        gens = []
                        if step >= 0:
                            gens += [chain(step, 0), chain(step, 1)]
                        if step + 1 < NTILES and step + 1 > 0:
                            gens += [shared(step + 1)]
                        if step + 1 < NTILES:
                            gens += [prep(step + 1, 0), prep(step + 1, 1)]
                        for _ in itertools.zip_longest(*gens):
                            pass
                    P.fence()
                    for hh in range(2):
                        DMA("sp", o_sdr[heads[hh]], S32[hh][:], R=[f"S32_{hh}"], W=["o_sdr"])
            P.fence()
        while copy_chunks:
            da_, sa_ = copy_chunks.pop()
            DMA("sp", da_, sa_, W=["o_sw"])
        esX.close()
        P.fence()
        if stop_after not in ("att", "dn", "dnpre"):
            PC0, PC1 = 126, P2N - 127
            TB2 = [(126, 512), (638, 512), (1150, 3)]
            h2bf = sb("h2bf", [128, KC, P2N], BF16)
            r2 = sb("r2", [128, P2N])
            lnp = sb("lnp_s", [128, 3 * KC])
            DMA("sp", lnp[:], lnp_d, W=["lnp"])

            def rstd_from(acc_banks, dst, scale_div):
                for bi_, (c0, n) in enumerate(TB2):
                    bk = psb[5 + bi_]
                    ACTF(dst[:, c0:c0 + n], bk[:, 0:n], AF.Sqrt, R=["eps"], W=[f"ps{5 + bi_}", "r2"], bias=eps_t[:], scale=1.0 / scale_div)
                RCP(dst[:, PC0:PC1], dst[:, PC0:PC1], W=["r2"])

            with ExitStack() as esP:
                sbp = lambda name, shape, dt=F32: esP.enter_context(nc.sbuf_tensor(name, list(shape), dt))
                mixb = sbp("mixb", [128, KC, P2N], BF16)
                X1 = sbp("X1", [128, KC, P2N])
                wo = [sbp(f"wo{i}", [128, KC, 128], BF16) for i in range(2)]
                sq2 = [sbp(f"sq2_{i}", [128, 512], BF16) for i in range(2)]
                xs = [sbp(f"xs{i}", [128, P2N]) for i in range(2)]
                for kc in range(KC):
                    DMA("sp" if kc % 2 == 0 else "act", mixb[:, kc, :], mixD[128 * kc:128 * kc + 128, :], R=["mixD"], W=[f"mixb{kc}"])
                mixb_all = [f"mixb{kc}" for kc in range(KC)]
                nsq = 0
                for ob in range(KC):
                    wi = ob % 2
                    DMA("pool", wo[wi][:], w_out[ob], W=[f"wo{wi}"])
                    for bi_, (c0, n) in enumerate(TB2):
                        for kc in range(KC):
                            MM(psb[bi_][:, 0:n], wo[wi][:, kc, :], mixb[:, kc, c0:c0 + n], start=(kc == 0), stop=(kc == KC - 1), R=[f"wo{wi}"] + mixb_all, W=[f"ps{bi_}"])
                        CP(X1[:, ob, c0:c0 + n], psb[bi_][:, 0:n], W=[f"ps{bi_}", f"X1_{ob}"], eng="act")
                        si = nsq % 2
                        nsq += 1
                        ACTF(sq2[si][:, 0:n], X1[:, ob, c0:c0 + n], AF.Square, R=[f"X1_{ob}"], W=[f"sq2_{si}"])
                        MM(psb[5 + bi_][:, 0:n], ones_bf[:], sq2[si][:, 0:n], start=(ob == 0), stop=(ob == KC - 1), R=[f"sq2_{si}", "ones_bf"], W=[f"ps{5 + bi_}"])
                rstd_from(None, r2, D)
                def p2_load(ob):
                    DMA("sp", xs[ob % 2][:, PC0:PC1], xT[128 * ob:128 * ob + 128, P2_0 + PC0:P2_0 + PC1], W=[f"xs{ob % 2}"])

                def p2_comp(ob):
                    nonlocal nsq
                    xi = ob % 2
                    STT(X1[:, ob, PC0:PC1], X1[:, ob, PC0:PC1], lnp[:, ob:ob + 1], r2[:, PC0:PC1], ALU.mult, ALU.mult, R=["lnp", "r2"], W=[f"X1_{ob}"])
                    TT(X1[:, ob, PC0:PC1], X1[:, ob, PC0:PC1], xs[xi][:, PC0:PC1], ALU.add, R=[f"xs{xi}"], W=[f"X1_{ob}"])
                    DMA("act", x1D[128 * ob:128 * ob + 128, 128:PC1], X1[:, ob, 128:PC1], R=[f"X1_{ob}"], W=["x1D"])
                    for bi_, (c0, n) in enumerate(TB2):
                        si = nsq % 2
                        nsq += 1
                        ACTF(sq2[si][:, 0:n], X1[:, ob, c0:c0 + n], AF.Square, R=[f"X1_{ob}"], W=[f"sq2_{si}"])
                        MM(psb[5 + bi_][:, 0:n], ones_bf[:], sq2[si][:, 0:n], start=(ob == 0), stop=(ob == KC - 1), R=[f"sq2_{si}", "ones_bf"], W=[f"ps{5 + bi_}"])

                p2_load(0)
                for ob in range(KC):
                    if ob + 1 < KC:
                        p2_load(ob + 1)
                    p2_comp(ob)
                rstd_from(None, r2, D)
                for ob in range(KC):
                    STT(h2bf[:, ob, PC0:PC1], X1[:, ob, PC0:PC1], lnp[:, KC + ob:KC + ob + 1], r2[:, PC0:PC1], ALU.mult, ALU.mult, R=[f"X1_{ob}", "lnp", "r2"], W=[f"h2bf{ob}"])
            P.fence()
            h2_all = [f"h2bf{ob}" for ob in range(KC)]

            P.mark('p2_pass3')
            NJ = DFF // 128
            with ExitStack() as esF:
                sbf_ = lambda name, shape, dt=F32: esF.enter_context(nc.sbuf_tensor(name, list(shape), dt))
                YC0, YC1 = 128, P2N - 127
                YN = YC1 - YC0
                hid = sbf_("hid", [128, NJ, YN], BF16)
                esG = ExitStack()
                sbg = lambda name, shape, dt=F32: esG.enter_context(nc.sbuf_tensor(name, list(shape), dt))
                wg = [sbg(f"wg{i}", [128, KC, 128], BF16) for i in range(2)]
                wv = [sbg(f"wv{i}", [128, KC, 128], BF16) for i in range(2)]
                upg = sbg("upg", [128, P2N + 2])
                upv = sbg("upv", [128, P2N + 2])
                cg = sbg("cg", [128, P2N])
                cv = sbg("cv", [128, P2N])
                ug = sbg("ug", [128, P2N])
                fcw = sbg("fcw_s", [128, 2 * NJ, 3])
                fcb = sbg("fcb_s", [128, 2 * NJ])
                fst = sbg("fst_s", [128, 2 * NJ, 2])
                ysm = sbg("ysm", [128, 2])
                DMA("sp", fcw[:].rearrange("p a b -> p (a b)"), fcw_d, W=["fcw"])
                DMA("sp", fcb[:], fcb_d, W=["fcb"])
                DMA("sp", fst[:].rearrange("p a b -> p (a b)"), fst_d, W=["fst"])
                MSET(upg[:, 0:2], 0.0, W=["upg"])
                MSET(upv[:, 0:2], 0.0, W=["upv"])
                GC = 2.0 * math.sqrt(2.0 / math.pi)
                SC = P2N - 128
                for j in range(NJ):
                    wi = j % 2
                    DMA("pool", wg[wi][:], w_ffn_in[j], W=[f"wg{wi}"])
                    DMA("pool", wv[wi][:], w_ffn_in[DFF // 128 + j], W=[f"wv{wi}"])
                    for gv, (wtile, wn, up, upn) in enumerate(((wg[wi], f"wg{wi}", upg, "upg"), (wv[wi], f"wv{wi}", upv, "upv"))):
                        for bi_, (c0, n) in enumerate(TB2):
                            pbi = 3 * gv + bi_
                            for kc in range(KC):
                                MM(psb[pbi][:, 0:n], wtile[:, kc, :], h2bf[:, kc, c0:c0 + n], start=(kc == 0), stop=(kc == KC - 1), R=[wn] + h2_all, W=[f"ps{pbi}"])
                            CP(up[:, 2 + c0:2 + c0 + n], psb[pbi][:, 0:n], W=[f"ps{pbi}", upn], eng="act")
                        jj = gv * NJ + j
                        DMA("sp", o_pfc[gv, j], up[:, 2 + SC - 2:2 + SC], R=[upn], W=["o_pfc"])
                        DMA("sp", o_sfc[1:2, gv * DFF + 128 * j:gv * DFF + 128 * j + 128].rearrange("a d -> d a"), up[:, 2 + SC:2 + SC + 1], R=[upn], W=["o_sfc"])
                        cdst = cg if gv == 0 else cv
                        cn = "cg" if gv == 0 else "cv"
                        TS(cdst[:, YC0:YC1], up[:, YC0:YC1], fcw[:, jj, 0:1], fcb[:, jj:jj + 1], ALU.mult, ALU.add, R=[upn, "fcw", "fcb"], W=[cn])
                        STT(cdst[:, YC0:YC1], up[:, YC0 + 1:YC1 + 1], fcw[:, jj, 1:2], cdst[:, YC0:YC1], ALU.mult, ALU.add, R=[upn, "fcw"], W=[cn])
                        STT(cdst[:, YC0:YC1], up[:, YC0 + 2:YC1 + 2], fcw[:, jj, 2:3], cdst[:, YC0:YC1], ALU.mult, ALU.add, R=[upn, "fcw"], W=[cn])
                        TS(ysm[:, 0:1], fst[:, jj, 0:1], fcw[:, jj, 0:1], fcb[:, jj:jj + 1], ALU.mult, ALU.add, R=["fst", "fcw", "fcb"], W=["ysm"])
                        STT(ysm[:, 0:1], fst[:, jj, 1:2], fcw[:, jj, 1:2], ysm[:, 0:1], ALU.mult, ALU.add, R=["fst", "fcw"], W=["ysm"])
                        STT(cdst[:, SC:SC + 1], up[:, 2 + SC:2 + SC + 1], fcw[:, jj, 2:3], ysm[:, 0:1], ALU.mult, ALU.add, R=[upn, "fcw", "ysm"], W=[cn])
                    ACTF(ug[:, YC0:YC1], cg[:, YC0:YC1], AF.Square, R=["cg"], W=["ug"])
                    TS(ug[:, YC0:YC1], ug[:, YC0:YC1], 0.044715, 1.0, ALU.mult, ALU.add, W=["ug"])
                    TT(ug[:, YC0:YC1], ug[:, YC0:YC1], cg[:, YC0:YC1], ALU.mult, R=["cg"], W=["ug"])
                    ACTF(ug[:, YC0:YC1], ug[:, YC0:YC1], AF.Sigmoid, W=["ug"], scale=GC)
                    TT(ug[:, YC0:YC1], ug[:, YC0:YC1], cg[:, YC0:YC1], ALU.mult, R=["cg"], W=["ug"])
                    TT(hid[:, j, :], ug[:, YC0:YC1], cv[:, YC0:YC1], ALU.mult, R=["ug", "cv"], W=[f"hid{j}"])
                hid_all = [f"hid{j}" for j in range(NJ)]
                P.mark('ffn_in_done')
                esG.close()
                P.fence()
                TBY = [(0, 512), (512, 512), (1024, 1)]
                fA = h2bf[:, :, :].rearrange("p a b -> p (a b)").bitcast(F32)
                NA = (KC * P2N // 2) // YN
                fB = sbf_("fB", [128, KC - NA, YN])
                fblk = lambda ob: (fA[:, YN * ob:YN * ob + YN] if ob < NA else fB[:, ob - NA, :])
                cvf = [sbf_(f"cvf{i}", [128, YN]) for i in range(2)]
                wf = [sbf_(f"wf{i}", [128, NJ, 128], BF16) for i in range(2)]
                sq4 = [sbf_(f"sq4_{i}", [128, 512], BF16) for i in range(2)]
                nf = 0
                for ob in range(KC):
                    wi = ob % 2
                    DMA("pool", wf[wi][:], w_ffn_out[ob], W=[f"wf{wi}"])
                    fb = fblk(ob)
                    for bi_, (c0, n) in enumerate(TBY):
                        for kc in range(NJ):
                            MM(psb[bi_][:, 0:n], wf[wi][:, kc, :], hid[:, kc, c0:c0 + n], start=(kc == 0), stop=(kc == NJ - 1), R=[f"wf{wi}"] + hid_all, W=[f"ps{bi_}"])
                        fi = nf % 2
                        nf += 1
                        CP(fb[:, c0:c0 + n], psb[bi_][:, 0:n], W=[f"ps{bi_}", f"f{ob}"], eng="act")
                        ACTF(sq4[fi][:, 0:n], fb[:, c0:c0 + n], AF.Square, R=[f"f{ob}"], W=[f"sq4_{fi}"])
                        MM(psb[5 + bi_][:, 0:n], ones_bf[:], sq4[fi][:, 0:n], start=(ob == 0), stop=(ob == KC - 1), R=[f"sq4_{fi}", "ones_bf"], W=[f"ps{5 + bi_}"])
                for bi_, (c0, n) in enumerate(TBY):
                    ACTF(r2[:, c0:c0 + n], psb[5 + bi_][:, 0:n], AF.Sqrt, R=["eps"], W=[f"ps{5 + bi_}", "r2"], bias=eps_t[:], scale=1.0 / D)
                RCP(r2[:, 0:YN], r2[:, 0:YN], W=["r2"])

                def fin_load(ob):
                    DMA("act" if ob % 2 == 0 else "sp", cvf[ob % 2][:], x1D[128 * ob:128 * ob + 128, YC0:YC1], R=["x1D"], W=[f"cvf{ob % 2}"])

                def fin_comp(ob):
                    fb = fblk(ob)
                    STT(fb, fb, lnp[:, 2 * KC + ob:2 * KC + ob + 1], r2[:, 0:YN], ALU.mult, ALU.mult, R=["lnp", "r2"], W=[f"f{ob}"])
                    TT(fb, fb, cvf[ob % 2][:], ALU.add, R=[f"cvf{ob % 2}"], W=[f"f{ob}"])
                    DMA("sp" if ob % 2 == 0 else "act", o_y[128 * ob:128 * ob + 128, :], fb, R=[f"f{ob}"], W=["o_y"])

                fin_load(0)
                for ob in range(KC):
                    if ob + 1 < KC:
                        fin_load(ob + 1)
                    fin_comp(ob)
        if debug:
            dbg = sb("dbg", [128, 256], BF16)
            dbg32 = sb("dbg32", [128, 256], F32)
            for rb_ in range(16):
                for ck in range(5):
                    DMA("sp", dbg[:], mixD[128 * rb_:128 * rb_ + 128, 256 * ck:256 * ck + 256], R=["mixD"], W=["dbg"])
                    CP(dbg32[:], dbg[:], R=["dbg"], W=["dbg32"])
                    DMA("sp", o_mix[128 * rb_:128 * rb_ + 128, 256 * ck:256 * ck + 256], dbg32[:], R=["dbg32"], W=["o_mix"])
        P.mark('end')
        P.finish()
    nc._marks = P.marks
    return nc


def tile_weight(w):
    K_, C_ = w.shape
    Cp = -(-C_ // 128) * 128
    if Cp != C_:
        w = np.concatenate([w, np.zeros((K_, Cp - C_), w.dtype)], axis=1)
    return np.ascontiguousarray(w.reshape(K_ // 128, 128, Cp // 128, 128).transpose(2, 1, 0, 3))


def prep_weights(inputs):
    return {n: tile_weight(np.asarray(inputs[n][0])) for n in ("w_in", "w_out", "w_ffn_in", "w_ffn_out")}


def prep_core_inputs(c, inputs, consts, consts_w=None):
    if consts_w is None:
        consts_w = prep_weights(inputs)
    cm, ch, oh, negv = consts
    b, j = c // 4, c % 4
    pad = (3 - j) * 1024
    xv = np.zeros((NT, D), np.float32)
    xv[pad:VT] = inputs["x_prompt"][b, 0:VT - pad]
    xv[VT] = inputs["x_sample"][c, 0]
    kmb = np.zeros((128, 96), np.float32)
    p = np.arange(128)
    for bi, dil in enumerate(DILS):
        for r in range(dil):
            for n in range(32 // dil):
                tok = dil * (128 * n + p) + r
                col = {0: n, 1: 32 + r * 8 + n, 2: 64 + r * 2 + n}[bi]
                kmb[:, col] = np.where(tok < pad, NEG, 0.0)
    gmask = np.ones((128, NTILES), np.float32)
    gmask[1:, NTILES - 1] = 0.0
    fm = lambda a: np.ascontiguousarray(a.reshape(a.shape[0], -1, 128).transpose(2, 1, 0).reshape(128, -1))
    return {
        "xT": np.ascontiguousarray(xv.T),
        "kmb": kmb,
        "cm": cm, "oh": oh, "negv": negv, "chm": ch, "gmask": gmask,
        "rel_bias": np.ascontiguousarray(inputs["rel_bias"]),
        "ln1": np.ascontiguousarray(inputs["ln_mix_pre"][0].reshape(KC, 128).T),
        "w_in": consts_w["w_in"],
        "cache_k": np.ascontiguousarray(inputs["cache_win_k"][0, c].reshape(2048, 1024)),
        "cache_v": np.ascontiguousarray(inputs["cache_win_v"][0, c].reshape(2048, 1024)),
        "st_dnc": np.ascontiguousarray(inputs["state_dn_conv"][0, c]),
        "cw": fm(inputs["dn_conv_w"][0]),
        "stf": fm(inputs["state_dn_conv"][0, c]),
        "alog": np.ascontiguousarray(inputs["dn_A_log"][0].reshape(1, 8)),
        "dtb": np.ascontiguousarray(inputs["dn_dt_bias"][0].reshape(1, 8)),
        "nw": np.ascontiguousarray(inputs["dn_norm_w"][0].reshape(1, 128)),
        "st_rec": np.ascontiguousarray(inputs["state_dn_rec"][0, c]),
        "w_out": consts_w["w_out"],
        "w_ffn_in": consts_w["w_ffn_in"],
        "w_ffn_out": consts_w["w_ffn_out"],
        "lnp": np.ascontiguousarray(np.concatenate([inputs[n][0].reshape(KC, 128).T for n in ("ln_mix_post", "ln_ffn_pre", "ln_ffn_post")], axis=1)),
        "fcw": fm(inputs["ffn_conv_w"][0]),
        "fcb": fm(inputs["ffn_conv_b"][0].reshape(1, -1)),
        "fst": fm(inputs["state_ffn_conv"][0, c]),
        "st_ffc": np.ascontiguousarray(inputs["state_ffn_conv"][0, c]),
    }


def kernel(**inputs):
    inputs = {k: np.asarray(v) for k, v in inputs.items()}
    consts = host_consts()
    nc = build(debug=False)
    consts_w = prep_weights(inputs)
    in_maps = [prep_core_inputs(c, inputs, consts, consts_w) for c in range(NCORES)]
    res = run_bass_kernel_spmd(nc, in_maps, core_ids=list(range(NCORES)))
    R = res.results
    f32 = np.float32
    y_prompt = np.zeros((2, 4096, D), f32)
    y_sample = np.zeros((8, 1, D), f32)
    p_wk = np.zeros((1, 2, 2048, 8, 128), f32)
    p_wv = np.zeros((1, 2, 2048, 8, 128), f32)
    p_dc = np.zeros((1, 2, 3, 2048), f32)
    p_dr = np.zeros((1, 2, 8, 128, 128), f32)
    p_fc = np.zeros((1, 2, 2, 2 * DFF), f32)
    s_wk = np.zeros((1, 8, 2048, 8, 128), f32)
    s_wv = np.zeros((1, 8, 2048, 8, 128), f32)
    s_dc = np.zeros((1, 8, 3, 2048), f32)
    s_dr = np.zeros((1, 8, 8, 128, 128), f32)
    s_fc = np.zeros((1, 8, 2, 2 * DFF), f32)
    for b in range(2):
        r = R[4 * b + 3]
        p_wk[0, b] = np.transpose(r["o_wk"], (2, 0, 1))
        p_wv[0, b] = np.transpose(r["o_wv"], (2, 0, 1))
        p_dc[0, b] = np.transpose(r["o_dnc"][:, :, 0:3], (2, 0, 1)).reshape(3, 2048)
        p_dr[0, b] = r["o_pdr"]
        p_fc[0, b] = np.transpose(r["o_pfc"], (3, 0, 1, 2)).reshape(2, 2 * DFF)
    for c in range(NCORES):
        s_wk[0, c] = R[c]["o_swk"].reshape(2048, 8, 128)
        s_wv[0, c] = R[c]["o_swv"].reshape(2048, 8, 128)
        s_dc[0, c] = R[c]["o_sdc"]
        s_dr[0, c] = R[c]["o_sdr"]
        s_fc[0, c] = R[c]["o_sfc"]
        yT = R[c]["o_y"]
        y_prompt[c // 4, 1024 * (c % 4):1024 * (c % 4) + 1024] = yT[:, 0:1024].T
        y_sample[c, 0] = yT[:, 1024]
    return (y_prompt, y_sample, p_wk, p_wv, p_dc, p_dr, p_fc, s_wk, s_wv, s_dc, s_dr, s_fc)
```
